# Optimizing a Trainium2 kernel written in Bass

```python
import jax, jax.numpy as jnp
from jax import lax
import numpy as np

D_MODEL = 1024
BATCH = 8
SEQ = 2048
DEPTH = 4
DEC_BATCH = 128
DEC_SEQ = 4
PAST_LEN = 16384
PAGE_SIZE = 128

N_MIXERS = 2
N_CONV = (DEPTH + 1) // 2
N_POOL = DEPTH // 2
CONV_W = 3
POOL_WINDOWS = (2, 4, 8, 16)
N_POOL_GROUPS = len(POOL_WINDOWS)
POOL_GROUP = D_MODEL // N_POOL_GROUPS
POOL_HIST = max(POOL_WINDOWS) - 1
D_FF = 2816
N_NORMS = 6
EPS = 1e-6

kernel_name = "macaron_conv_pool_hybrid_step"


def rmsnorm(x, g):
    xf = x.astype(jnp.float32)
    y = xf * lax.rsqrt(jnp.mean(xf * xf, axis=-1, keepdims=True) + EPS)
    return (y * g.astype(jnp.float32)).astype(x.dtype)


def swiglu(h, w_gate, w_up, w_down):
    a = jnp.einsum('bsd,df->bsf', h, w_gate)
    b = jnp.einsum('bsd,df->bsf', h, w_up)
    return jnp.einsum('bsf,fd->bsd', jax.nn.silu(a) * b, w_down)


def conv_mixer(u, hist, w_in, kernel, w_out):
    s = u.shape[1]
    bcv = jnp.einsum('bsd,de->bse', u, w_in)
    gate_b, gate_c, v = jnp.split(bcv, 3, axis=-1)
    z = gate_c * v
    zf = jnp.concatenate([hist.astype(z.dtype), z], axis=1)
    conv = sum(kernel[k] * zf[:, k:k + s] for k in range(CONV_W))
    y = jnp.einsum('bsd,de->bse', gate_b * conv, w_out)
    return y, zf[:, -(CONV_W - 1):]


def pool_mixer(u, hist, start_pos, w_group, scale):
    s = u.shape[1]
    full = jnp.concatenate([hist.astype(u.dtype), u], axis=1)
    ff = full.astype(jnp.float32)
    cs = jnp.concatenate([jnp.zeros_like(ff[:, :1]), jnp.cumsum(ff, axis=1)], axis=1)
    pos = start_pos + jnp.arange(s)
    uf = u.astype(jnp.float32)
    outs = []
    for g, w in enumerate(POOL_WINDOWS):
        sl = slice(g * POOL_GROUP, (g + 1) * POOL_GROUP)
        lo = POOL_HIST + 1
        win_sum = cs[:, lo:lo + s, sl] - cs[:, lo - w:lo - w + s, sl]
        count = jnp.minimum(pos + 1, w).astype(jnp.float32)[None, :, None]
        diff = win_sum / count - uf[:, :, sl]
        outs.append(jnp.einsum('bsc,cd->bsd', diff.astype(u.dtype), w_group[g]))
    y = jnp.concatenate(outs, axis=-1) * scale
    return y, full[:, -POOL_HIST:]


def trunk(x, conv_hist, pool_hist, start_pos, norm_gains, ffn_w_gate, ffn_w_up, ffn_w_down,
          conv_w_in, conv_kernel, conv_w_out, pool_w_group, pool_scale):
    new_conv, new_pool = [], []
    for i in range(DEPTH):
        g = norm_gains[i]
        h = rmsnorm(x, g[0])
        x = x + 0.5 * rmsnorm(swiglu(h, ffn_w_gate[i, 0], ffn_w_up[i, 0], ffn_w_down[i, 0]), g[1])
        h = rmsnorm(x, g[2])
        j = i // N_MIXERS
        if i % N_MIXERS == 0:
            m, nh = conv_mixer(h, conv_hist[j], conv_w_in[j], conv_kernel[j], conv_w_out[j])
            new_conv.append(nh)
        else:
            m, nh = pool_mixer(h, pool_hist[j], start_pos, pool_w_group[j], pool_scale[j])
            new_pool.append(nh)
        x = x + rmsnorm(m, g[3])
        h = rmsnorm(x, g[4])
        x = x + 0.5 * rmsnorm(swiglu(h, ffn_w_gate[i, 1], ffn_w_up[i, 1], ffn_w_down[i, 1]), g[5])
    return x, jnp.stack(new_conv), jnp.stack(new_pool)


def setup_inputs(seed: int = 0) -> dict:
    key = jax.random.key(seed)
    ks = jax.random.split(key, 14)
    f32 = jnp.float32
    nrm = lambda k, shape, sc: jax.random.normal(k, shape, f32) * sc
    return {
        "x_prompt": nrm(ks[0], (BATCH, SEQ, D_MODEL), 1.0),
        "x_sample": nrm(ks[1], (DEC_BATCH, DEC_SEQ, D_MODEL), 1.0),
        "state_conv": nrm(ks[2], (N_CONV, DEC_BATCH, CONV_W - 1, D_MODEL), 1.0),
        "state_pool": nrm(ks[3], (N_POOL, DEC_BATCH, POOL_HIST, D_MODEL), 1.0),
        "norm_gains": 1.0 + nrm(ks[4], (DEPTH, N_NORMS, D_MODEL), 0.1),
        "ffn_w_gate": nrm(ks[5], (DEPTH, 2, D_MODEL, D_FF), D_MODEL ** -0.5),
        "ffn_w_up": nrm(ks[6], (DEPTH, 2, D_MODEL, D_FF), D_MODEL ** -0.5),
        "ffn_w_down": nrm(ks[7], (DEPTH, 2, D_FF, D_MODEL), D_FF ** -0.5),
        "conv_w_in": nrm(ks[8], (N_CONV, D_MODEL, 3 * D_MODEL), D_MODEL ** -0.5),
        "conv_kernel": nrm(ks[9], (N_CONV, CONV_W, D_MODEL), CONV_W ** -0.5),
        "conv_w_out": nrm(ks[10], (N_CONV, D_MODEL, D_MODEL), D_MODEL ** -0.5),
        "pool_w_group": nrm(ks[11], (N_POOL, N_POOL_GROUPS, POOL_GROUP, POOL_GROUP), POOL_GROUP ** -0.5),
        "pool_scale": 1.0 + nrm(ks[12], (N_POOL, D_MODEL), 0.1),
    }


def reference(x_prompt, x_sample, state_conv, state_pool, norm_gains, ffn_w_gate, ffn_w_up,
              ffn_w_down, conv_w_in, conv_kernel, conv_w_out, pool_w_group, pool_scale):
    conv_hist_p = jnp.zeros((N_CONV, x_prompt.shape[0], CONV_W - 1, D_MODEL), x_prompt.dtype)
    pool_hist_p = jnp.zeros((N_POOL, x_prompt.shape[0], POOL_HIST, D_MODEL), x_prompt.dtype)
    y_prompt, new_conv_prompt, new_pool_prompt = trunk(
        x_prompt, conv_hist_p, pool_hist_p, 0, norm_gains, ffn_w_gate, ffn_w_up, ffn_w_down,
        conv_w_in, conv_kernel, conv_w_out, pool_w_group, pool_scale)
    y_sample, new_conv_sample, new_pool_sample = trunk(
        x_sample, state_conv, state_pool, PAST_LEN, norm_gains, ffn_w_gate, ffn_w_up, ffn_w_down,
        conv_w_in, conv_kernel, conv_w_out, pool_w_group, pool_scale)
    return (y_prompt, y_sample, new_conv_prompt, new_conv_sample, new_pool_prompt, new_pool_sample)
```

```python
import collections
import numpy as np
import concourse.bass as bass
import concourse.mybir as mybir
from concourse.bass_utils import run_bass_kernel_spmd

F32 = mybir.dt.float32
BF16 = mybir.dt.bfloat16
AF = mybir.ActivationFunctionType
ALU = mybir.AluOpType

D = 1024
DFF = 2816
NCH = 8
NFC = 22
NPR = 1024
NSS = 8
NSM = 32
NTOK = NPR + NSM
TS = 352
NTILE = 3
NPASS = 2
GROUPS = [5, 5, 4, 4, 4]
GMAX = 5
RING = 10
SLOTW = 3072
EPS = 1e-6
N_CORES = 8
SLOTS_FULL = 4 * 44 + 2 * 11 + 2 * 1


def slots_per_pass(n_layers):
    return sum(44 + (11 if l % 2 == 0 else 1) for l in range(n_layers))

CFG = {"n_layers": 4, "npass": 2, "nsub": 12, "interleave": True}


class Prog:
    def __init__(self, nc):
        self.nc = nc
        self.eng = {}
        for name in ("pe", "act", "dve", "pool", "sp"):
            self.eng[name] = dict(sem=nc.alloc_semaphore("s_" + name), count=0, waited={}, ops=[])
        self.lastw = {}
        self.readers = {}
        self.dma_cnt = {}
        self.final_tokens = {}
        self.defer = False
        self.pending = collections.deque()
        self.done = set()
        self.fence_next = False

    def pe_fence(self):
        self.fence_next = True

    def drain_step(self, nbulk=4):
        n = 0
        while self.pending:
            item = self.pending[0]
            if item[0] == "M":
                self.pending.popleft()
                self.done.add(item[1])
                continue
            if item[0] == "L":
                if n > 0:
                    break
                self.pending.popleft()
                self._op(*item[1], **item[2])
                break
            self.pending.popleft()
            self._op(*item[1], **item[2])
            n += 1
            if n >= nbulk:
                break

    def flush_through(self, marker):
        if marker in self.done:
            return
        assert any(it[0] == "M" and it[1] == marker for it in self.pending), marker
        while self.pending:
            item = self.pending.popleft()
            if item[0] == "M":
                self.done.add(item[1])
                if item[1] == marker:
                    return
                continue
            self._op(*item[1], **item[2])

    def flush_all(self):
        while self.pending:
            item = self.pending.popleft()
            if item[0] != "M":
                self._op(*item[1], **item[2])
            else:
                self.done.add(item[1])

    def mark(self, marker):
        self.pending.append(("M", marker))

    def op(self, eng, fn, reads=(), writes=(), dma_sem=None, final=False, tag="B"):
        if self.defer:
            self.pending.append((tag, (eng, fn), dict(reads=list(reads), writes=list(writes), dma_sem=dma_sem, final=final)))
            return None
        return self._op(eng, fn, reads, writes, dma_sem, final)

    def dma_sem(self, name):
        s = self.nc.alloc_semaphore(name)
        self.dma_cnt[s.num] = 0
        return s

    def _op(self, eng, fn, reads=(), writes=(), dma_sem=None, final=False):
        e = self.eng[eng]
        deps = {}

        def add(tok):
            if tok is None:
                return
            k = tok[0].num
            if k not in deps or deps[k][1] < tok[1]:
                deps[k] = tok

        for r in reads:
            add(self.lastw.get(r))
        for w in writes:
            add(self.lastw.get(w))
            for tok in self.readers.get(w, {}).values():
                add(tok)
        waits = []
        for k, (sem, val) in deps.items():
            if eng == "pe" and sem is e["sem"]:
                continue
            if e["waited"].get(k, 0) >= val:
                continue
            e["waited"][k] = val
            waits.append((sem, val))
        if eng == "pe" and self.fence_next:
            self.fence_next = False
            if e["count"] > 0:
                waits.append((e["sem"], e["count"]))
        if dma_sem is not None:
            self.dma_cnt[dma_sem.num] += 16
            tok = (dma_sem, self.dma_cnt[dma_sem.num])
            inc = (dma_sem, 16)
        else:
            e["count"] += 1
            tok = (e["sem"], e["count"])
            inc = (e["sem"], 1)
        e["ops"].append((waits, fn, inc))
        for r in reads:
            self.readers.setdefault(r, {})[tok[0].num] = tok
        for w in writes:
            self.lastw[w] = tok
            self.readers[w] = {}
        if final:
            self.final_tokens[tok[0].num] = tok
        return tok

    def emit(self):
        nc = self.nc
        finals = list(self.final_tokens.values())

        class FirstWait:
            def __init__(self, h):
                self._h = h
                self._pend = None

            def __getattr__(self, name):
                attr = getattr(self._h, name)
                if not callable(attr):
                    return attr

                def g(*a, **k):
                    ins = attr(*a, **k)
                    if self._pend is not None and hasattr(ins, "_wait_ge"):
                        s_, v_ = self._pend
                        self._pend = None
                        ins._wait_ge(s_, v_)
                    return ins
                return g

        def replay(name, h):
            px = FirstWait(h)
            for waits, fn, inc in self.eng[name]["ops"]:
                for s, v in waits[:-1]:
                    h.wait_ge(s, v)
                px._pend = waits[-1] if waits else None
                ins = fn(px)
                assert px._pend is None
                ins.then_inc(inc[0], inc[1])

        with nc.Block() as block:
            @block.tensor
            def _(h):
                replay("pe", h)

            @block.scalar
            def _(h):
                replay("act", h)

            @block.vector
            def _(h):
                replay("dve", h)

            @block.gpsimd
            def _(h):
                replay("pool", h)

            @block.sync
            def _(h):
                replay("sp", h)
                for s, v in finals:
                    h.wait_ge(s, v)


def v3(ap):
    return ap.rearrange("p (s k) -> p s k", k=4)


def tiles_of(c0, c1):
    return [t for t in range(NTILE) if c0 < (t + 1) * TS and c1 > t * TS]


def build_program(cfg):
    n_layers = cfg["n_layers"]
    npass = cfg["npass"]
    nc = bass.Bass("TRN2", target_bir_lowering=False)
    P = Prog(nc)
    SLOTS_PER_PASS = slots_per_pass(n_layers)
    _subs = []
    for l in range(n_layers):
        _subs += [22, 11 if l % 2 == 0 else 1, 22]
    SPP = max(1, sum(_subs[:cfg.get("nsub", 12)]))

    x_in = nc.dram_tensor("x_in", [NPASS, NTOK, D], F32, kind="ExternalInput").ap()
    sconv = nc.dram_tensor("sconv", [2, NPASS, 16, D], F32, kind="ExternalInput").ap()
    spool = nc.dram_tensor("spool", [2, NPASS, NSS, 15, D], F32, kind="ExternalInput").ap()
    wstream = nc.dram_tensor("wstream", [SLOTS_PER_PASS, 128, SLOTW], F32, kind="ExternalInput").ap()
    params_d = nc.dram_tensor("params", [128, 256], F32, kind="ExternalInput").ap()
    ident_d = nc.dram_tensor("ident", [128, 128], F32, kind="ExternalInput").ap()
    y_o = nc.dram_tensor("y_o", [NPASS, NTOK, D], F32, kind="ExternalOutput").ap()
    cs_p = nc.dram_tensor("cs_p", [2, 2, D], F32, kind="ExternalOutput").ap()
    cs_s = nc.dram_tensor("cs_s", [2, NPASS, 16, D], F32, kind="ExternalOutput").ap()
    ps_p = nc.dram_tensor("ps_p", [2, 15, D], F32, kind="ExternalOutput").ap()
    ps_s = nc.dram_tensor("ps_s", [2, NPASS, NSS, 15, D], F32, kind="ExternalOutput").ap()

    xT = nc.alloc_sbuf_tensor("xT", [128, NCH, NTOK], F32)
    hreg = nc.alloc_sbuf_tensor("hreg", [128, NCH * NTOK // 2], F32)
    hT = hreg.bitcast(BF16).reshape([128, NCH, NTOK])
    xstg = [hreg[:, i * 1024:(i + 1) * 1024] for i in range(4)]
    acc = nc.alloc_sbuf_tensor("acc", [128, NCH, NTOK], F32)
    b2 = nc.alloc_sbuf_tensor("b2", [128, NCH, NTOK], BF16)
    ring = nc.alloc_sbuf_tensor("ring", [128, RING, SLOTW], BF16)
    MTW = 5400
    mt = nc.alloc_sbuf_tensor("mt", [128, MTW], F32)
    sa = nc.alloc_sbuf_tensor("sa", [128, 2, TS], F32)
    gt = nc.alloc_sbuf_tensor("gt", [128, 2, GMAX, TS], BF16)
    params = nc.alloc_sbuf_tensor("params_sb", [128, 256], F32)
    params_h = nc.alloc_sbuf_tensor("params_h", [128, 256], F32)
    ident = nc.alloc_sbuf_tensor("ident_sb", [128, 128], F32)
    ones_bf = nc.alloc_sbuf_tensor("ones_bf", [128, 128], BF16)
    epst = nc.alloc_sbuf_tensor("epst", [128, 1], F32)
    dmy = nc.alloc_sbuf_tensor("dmy", [128, 8], F32)
    invc = nc.alloc_sbuf_tensor("invc", [128, 4, 16], F32)
    tmp = nc.alloc_sbuf_tensor("tmp", [128, 2, TS], F32)
    stg = nc.alloc_sbuf_tensor("stg", [128, D], F32)
    phist = nc.alloc_sbuf_tensor("phist", [128, NCH, NSS * 15], F32)
    chist = nc.alloc_sbuf_tensor("chist", [128, NCH, 16], F32)
    csamp = nc.alloc_sbuf_tensor("csamp", [128, NCH, 16], F32)
    convh = nc.alloc_sbuf_tensor("convh", [128, 2, NCH, 2], F32)
    poolh = nc.alloc_sbuf_tensor("poolh", [128, 2, NCH, 15], F32)

    ZFW = 2 + NPR + NSS * 6
    zf = [mt[:, i * ZFW:(i + 1) * ZFW] for i in range(2)]
    o = 2 * ZFW
    gcs = [mt[:, o + i * TS:o + (i + 1) * TS] for i in range(2)]
    o += 2 * TS
    tb = [mt[:, o + i * TS:o + (i + 1) * TS] for i in range(3)]
    o += 3 * TS
    assert o <= MTW
    UFW = 15 + NPR + NSS * 19
    uf = [mt[:, i * UFW:(i + 1) * UFW] for i in range(2)]
    wa = mt[:, 2 * UFW:3 * UFW]
    wb = mt[:, 3 * UFW:4 * UFW]
    UTW = 15 + TS + NSS * 19
    uft = [mt[:, i * UTW:(i + 1) * UTW] for i in range(2)]
    wbuf = {"pool": [mt[:, (2 + i) * UTW:(3 + i) * UTW] for i in range(2)],
            "dve": [mt[:, (4 + i) * UTW:(5 + i) * UTW] for i in range(2)]}
    tmpf = mt[:, 6 * UTW:6 * UTW + 16]
    assert 6 * UTW + 16 <= MTW

    psa = [nc.alloc_psum_tensor(f"psa{i}", [128, 512], F32) for i in range(4)]
    pso = [nc.alloc_psum_tensor(f"pso{i}", [128, 512], F32) for i in range(3)]
    pss = nc.alloc_psum_tensor("pss", [128, 512], F32)

    ring_sem = [P.dma_sem(f"ring{i}") for i in range(RING)]
    xstg_sem = [P.dma_sem(f"xstg{i}") for i in range(4)]
    stg_sem = P.dma_sem("stg")
    misc_sem = P.dma_sem("misc")
    misc2_sem = P.dma_sem("misc2")
    h2h_sem = P.dma_sem("h2h")

    cnt = {"psa": 0, "pso": 0, "tmp": 0, "xstg": 0}

    def next_psa():
        i = cnt["psa"] % 4
        cnt["psa"] += 1
        return i

    def next_pso():
        i = cnt["pso"] % 3
        cnt["pso"] += 1
        return i

    HKEYS = [("h", c, t) for c in range(NCH) for t in range(NTILE)]

    ws = {"issued": 0, "total": npass * SPP, "rel": set(), "ptr": 0}

    def issue_slot_loads(upto):
        while ws["issued"] < min(upto, ws["total"]):
            i = ws["issued"]
            s = i % RING
            src = wstream[i % SPP]
            P.op("pool", lambda g, s=s, src=src: g.dma_start(out=ring[:, s, :], in_=src),
                 reads=list(ws.get("extra", [])), writes=[("slot", s)], dma_sem=ring_sem[s])
            ws["issued"] += 1

    def release(idxs):
        ws["rel"].update(idxs)
        while ws["ptr"] in ws["rel"]:
            ws["rel"].discard(ws["ptr"])
            ws["ptr"] += 1
        issue_slot_loads(ws["ptr"] + RING)

    def slots_ready(upto):
        while ws["issued"] < min(upto, ws["total"]):
            yield ("slot", upto)

    P.op("sp", lambda h: h.dma_start(out=params[:], in_=params_d), writes=[("params",)], dma_sem=misc_sem)
    P.op("sp", lambda h: h.dma_start(out=ident[:], in_=ident_d), writes=[("ident",)], dma_sem=misc2_sem)
    P.op("dve", lambda v: v.memset(ones_bf[:], 1.0), writes=[("ones",)])
    P.op("dve", lambda v: v.memset(epst[:], EPS), writes=[("eps",)])
    P.op("dve", lambda v: v.memset(dmy[:], 1.0), writes=[("dmy",)])
    P.op("dve", lambda v: v.tensor_scalar(out=params_h[:], in0=params[:], scalar1=0.5, scalar2=None, op0=ALU.mult),
         reads=[("params",)], writes=[("params_h",)])
    for g in range(4):
        w = 2 << g
        P.op("dve", lambda v, g=g, w=w: v.memset(invc[:, g, :], 1.0 / w), writes=[("invc", g)])
        for i in range(min(w - 1, 15)):
            P.op("dve", lambda v, g=g, i=i: v.memset(invc[:, g, i:i + 1], 1.0 / (i + 1)), writes=[("invc", g)])
    P.op("dve", lambda v: v.memset(convh[:], 0.0), writes=[("convh", 0), ("convh", 1)])
    P.op("dve", lambda v: v.memset(poolh[:], 0.0), writes=[("poolh", j_, c_) for j_ in range(2) for c_ in range(NCH)])
    issue_slot_loads(2)

    def gcol(l, n, c):
        return (l * 6 + n) * 8 + c

    def mt_acquire():
        keys = [("xs", i_) for i_ in range(5)] + [("xs", i_, h_) for i_ in (3, 4) for h_ in (0, 1)]
        P.op("dve", lambda v: v.memset(dmy[:, 2:3], 0.0), reads=[("dmy",)], writes=keys)
        P.op("pool", lambda g_: g_.memset(dmy[:, 3:4], 0.0), reads=[("dmy",)], writes=keys)
        P.op("act", lambda a: a.activation(out=dmy[:, 4:5], in_=dmy[:, 6:7], func=AF.Copy), reads=[("dmy",)], writes=keys)

    def mt_release():
        P.op("dve", lambda v: v.memset(dmy[:, 2:3], 0.0), reads=[("dmy",)], writes=[("mtfree", "dve")])
        P.op("pool", lambda g_: g_.memset(dmy[:, 3:4], 0.0), reads=[("dmy",)], writes=[("mtfree", "pool")])
        P.op("act", lambda a: a.activation(out=dmy[:, 4:5], in_=dmy[:, 6:7], func=AF.Copy), reads=[("dmy",)], writes=[("mtfree", "act")])

    def norm_stats(src, skey, t):
        cols = slice(t * TS, (t + 1) * TS)
        for c in range(NCH):
            P.op("act", lambda a, c=c: a.activation(out=b2[:, c, cols], in_=src[:, c, cols], func=AF.Square),
                 reads=[(skey, c, t)], writes=[("b2", c, t)])

        def f(pe):
            for c in range(NCH):
                mm = pe.matmul(pss[:, 0:TS], lhsT=ones_bf[:], rhs=b2[:, c, cols], start=(c == 0), stop=(c == NCH - 1))
            return mm
        P.op("pe", f, reads=[("b2", c, t) for c in range(NCH)] + [("ones",)], writes=[("pss",)], tag="L")

        P.op("act", lambda a: a.activation(out=dmy[:, 0:1], in_=dmy[:, 6:7], func=AF.Ln), reads=[("dmy",)], writes=[("dmy0",)])
        P.op("act", lambda a: a.activation(out=pss[:, 0:TS], in_=pss[:, 0:TS], func=AF.Ln, scale=1.0 / D, bias=epst[:, 0:1]),
             reads=[("pss",), ("eps",)], writes=[("pss",)], tag="L")

        def fe(a):
            ins = a.activation(out=pss[:, 0:TS], in_=pss[:, 0:TS], func=AF.Exp, scale=-0.5)
            return ins
        P.op("act", fe, reads=[("pss",)], writes=[("pss",)], tag="L")

    def prenorm(l, n, t, to_u=False):
        cols = slice(t * TS, (t + 1) * TS)
        norm_stats(xT, "x", t)
        for c in range(NCH):
            dst = acc if to_u else hT
            dkey = "acc" if to_u else "h"
            gc_ = gcol(l, n, c)
            P.op("dve", lambda v, c=c, dst=dst, gc_=gc_: v.scalar_tensor_tensor(
                out=dst[:, c, cols], in0=xT[:, c, cols], scalar=params[:, gc_:gc_ + 1], in1=pss[:, 0:TS],
                op0=ALU.mult, op1=ALU.mult),
                reads=[("x", c, t), ("pss",), ("params",)], writes=[(dkey, c, t)])

    def postnorm(l, n, t, half):
        cols = slice(t * TS, (t + 1) * TS)
        norm_stats(acc, "acc", t)
        pt = params_h if half else params
        pk = ("params_h",) if half else ("params",)
        for c in range(NCH):
            gc_ = gcol(l, n, c)
            P.op("dve", lambda v, c=c, gc_=gc_: v.scalar_tensor_tensor(
                out=acc[:, c, cols], in0=acc[:, c, cols], scalar=pt[:, gc_:gc_ + 1], in1=pss[:, 0:TS],
                op0=ALU.mult, op1=ALU.mult),
                reads=[("acc", c, t), ("pss",), pk], writes=[("acc", c, t)])
            if c < 5:
                P.op("pool", lambda g_, c=c: g_.tensor_tensor(out=xT[:, c, cols], in0=xT[:, c, cols], in1=acc[:, c, cols], op=ALU.add),
                     reads=[("x", c, t), ("acc", c, t)], writes=[("x", c, t)])
        for c in range(5, NCH):
            P.op("dve", lambda v, c=c: v.tensor_tensor(out=xT[:, c, cols], in0=xT[:, c, cols], in1=acc[:, c, cols], op=ALU.add),
                 reads=[("x", c, t), ("acc", c, t)], writes=[("x", c, t)])

    def ffn(base_idx, finish, need, groups=GROUPS, prefix=1):
        gstart = [0]
        for gsz in groups:
            gstart.append(gstart[-1] + gsz)
        assert gstart[-1] == NFC
        units = [(gi, t) for t in range(NTILE) for gi in range(prefix)]
        units += [(gi, t) for gi in range(prefix, len(groups)) for t in range(NTILE)]
        qc = {"q": 0}

        def AB(ui):
            gi, t = units[ui]
            u = ui % 2
            cols = slice(t * TS, (t + 1) * TS)
            if gi == 0 and need is not None:
                yield ("need", need(t))
            yield from slots_ready(base_idx + gstart[gi + 1])
            for jj in range(groups[gi]):
                j = gstart[gi] + jj
                s = (base_idx + j) % RING
                q = qc["q"]
                qc["q"] += 1
                ia, ib = 2 * (q % 2), 2 * (q % 2) + 1

                def fa(pe, s=s, ia=ia, off=0, cols=cols):
                    for k in range(NCH):
                        mm = pe.matmul(psa[ia][:, 0:TS], lhsT=ring[:, s, off + k * 128:off + (k + 1) * 128],
                                       rhs=hT[:, k, cols], start=(k == 0), stop=(k == NCH - 1))
                    return mm
                P.op("pe", fa, reads=[("h", k, t) for k in range(NCH)] + [("slot", s)], writes=[("psa", ia)])
                P.op("pe", lambda pe, s=s, ib=ib, fa=fa: fa(pe, s, ib, 1024),
                     reads=[("h", k, t) for k in range(NCH)] + [("slot", s)], writes=[("psa", ib)])
                P.op("act", lambda a, ia=ia, q=q: a.activation(out=sa[:, q % 2, :], in_=psa[ia][:, 0:TS], func=AF.Silu),
                     reads=[("psa", ia)], writes=[("sa", q % 2)])
                P.op("dve", lambda v, ib=ib, q=q, u=u, jj=jj: v.tensor_tensor(
                    out=gt[:, u, jj, :], in0=psa[ib][:, 0:TS], in1=sa[:, q % 2, :], op=ALU.mult),
                    reads=[("psa", ib), ("sa", q % 2)], writes=[("g", u, jj)])
                yield None

        def OUT(ui):
            gi, t = units[ui]
            u = ui % 2
            cols = slice(t * TS, (t + 1) * TS)
            G = groups[gi]
            slots = [(base_idx + gstart[gi] + jj) % RING for jj in range(G)]
            for c in range(NCH):
                io = next_pso()

                def fo(pe, c=c, io=io, G=G, slots=slots, u=u):
                    for jj in range(G):
                        mm = pe.matmul(pso[io][:, 0:TS], lhsT=ring[:, slots[jj], 2048 + c * 128:2048 + (c + 1) * 128],
                                       rhs=gt[:, u, jj, :], start=(jj == 0), stop=(jj == G - 1))
                    return mm
                P.op("pe", fo, reads=[("g", u, jj) for jj in range(G)] + [("slot", s) for s in slots], writes=[("pso", io)])
                if gi == 0:
                    P.op("act", lambda a, c=c, io=io, cols=cols: a.activation(out=acc[:, c, cols], in_=pso[io][:, 0:TS], func=AF.Copy),
                         reads=[("pso", io)], writes=[("acc", c, t)])
                else:
                    P.op("dve", lambda v, c=c, io=io, cols=cols: v.tensor_tensor(out=acc[:, c, cols], in0=pso[io][:, 0:TS], in1=acc[:, c, cols], op=ALU.add),
                         reads=[("pso", io), ("acc", c, t)], writes=[("acc", c, t)])
                yield None
            if t == NTILE - 1:
                release(range(base_idx + gstart[gi], base_idx + gstart[gi + 1]))
            if gi == len(groups) - 1:
                finish(t)

        yield from AB(0)
        for ui in range(1, len(units)):
            yield from AB(ui)
            yield from OUT(ui - 1)
        yield from OUT(len(units) - 1)

    RB = [(rb * 128, 128) for rb in range(8)] + [(1024, 32)]
    xs = [mt[:, i * 1024:(i + 1) * 1024] for i in range(5)]
    xs_sem = [P.dma_sem(f"xs{i}") for i in range(5)]
    GUARD = [("mtfree", e_) for e_ in ("dve", "pool", "act")]
    xcnt = {"l": 0, "s": 0}

    def L_rb(p, k):
        r0, rows = RB[k]
        i = xcnt["l"] % 3
        xcnt["l"] += 1
        P.op("sp", lambda h, i=i, r0=r0, rows=rows: h.dma_start(out=xs[i][0:rows, :], in_=x_in[p, r0:r0 + rows, :]),
             reads=GUARD, writes=[("xs", i)], dma_sem=xs_sem[i])
        return i

    def T_rb(p, k, i):
        r0, rows = RB[k]
        tl = tiles_of(r0, r0 + rows)
        for half in range(2):
            io = next_pso()

            def ft(pe, i=i, io=io, half=half, rows=rows):
                for cc in range(4):
                    c = half * 4 + cc
                    mm = pe.transpose(out=pso[io][:, cc * 128:cc * 128 + rows], in_=xs[i][0:rows, c * 128:(c + 1) * 128],
                                      identity=ident[0:rows, 0:rows])
                return mm
            P.op("pe", ft, reads=[("xs", i), ("ident",)], writes=[("pso", io)])
            P.op("act", lambda a, io=io, half=half, r0=r0, rows=rows: a.activation(
                out=xT[:, half * 4:half * 4 + 4, r0:r0 + rows],
                in_=pso[io][:, :].rearrange("p (a b) -> p a b", a=4)[:, :, 0:rows], func=AF.Copy),
                reads=[("pso", io)], writes=[("x", half * 4 + cc, t) for cc in range(4) for t in tl])

    def S_rb(p, k):
        r0, rows = RB[k]
        tl = tiles_of(r0, r0 + rows)
        i = 3 + xcnt["s"] % 2
        xcnt["s"] += 1
        for half in range(2):
            io = next_pso()

            def ft(pe, io=io, half=half, r0=r0, rows=rows):
                for cc in range(4):
                    c = half * 4 + cc
                    mm = pe.transpose(out=pso[io][0:rows, cc * 128:(cc + 1) * 128], in_=xT[:, c, r0:r0 + rows], identity=ident[:])
                return mm
            P.op("pe", ft, reads=[("x", half * 4 + cc, t) for cc in range(4) for t in tl] + [("ident",)], writes=[("pso", io)])
            P.op("act", lambda a, i=i, io=io, half=half, rows=rows: a.activation(
                out=xs[i][0:rows, half * 512:(half + 1) * 512], in_=pso[io][0:rows, :], func=AF.Copy),
                reads=[("pso", io)] + GUARD, writes=[("xs", i, half)])
        P.op("sp", lambda h, i=i, r0=r0, rows=rows: h.dma_start(out=y_o[p, r0:r0 + rows, :], in_=xs[i][0:rows, :]),
             reads=[("xs", i, 0), ("xs", i, 1)], writes=[("xs", i)], dma_sem=xs_sem[i], final=True)

    def boundary(p_store, p_load):
        bufs = {}
        if p_load is not None:
            for k in range(3):
                bufs[k] = L_rb(p_load, k)
        for k in range(len(RB)):
            if p_store is not None:
                S_rb(p_store, k)
            if p_load is not None:
                T_rb(p_load, k, bufs[k])
                if k + 3 < len(RB):
                    bufs[k + 3] = L_rb(p_load, k + 3)

    def transposed_out(src_fn, rows, dst_aps, rkeys):
        for half in range(2):
            io = next_pso()

            def ft(pe, io=io, half=half):
                for cc in range(4):
                    mm = pe.transpose(out=pso[io][0:rows, cc * 128:(cc + 1) * 128], in_=src_fn(half * 4 + cc), identity=ident[:])
                return mm
            P.pe_fence()
            P.op("pe", ft, reads=list(rkeys) + [("ident",)], writes=[("pso", io)])
            P.pe_fence()
            P.op("act", lambda a, io=io, half=half: a.activation(out=stg[0:rows, half * 512:(half + 1) * 512], in_=pso[io][0:rows, :], func=AF.Copy),
                 reads=[("pso", io)], writes=[("stg", half)])
        for (dap, r0, nr) in dst_aps:
            P.op("sp", lambda h, dap=dap, r0=r0, nr=nr: h.dma_start(out=dap, in_=stg[r0:r0 + nr, :]),
                 reads=[("stg", 0), ("stg", 1)], writes=[("stgdma",)], dma_sem=stg_sem, final=True)

    def load_hist(src_ap, rows, dst, dkey):
        P.op("sp", lambda h: h.dma_start(out=stg[0:rows, :], in_=src_ap),
             writes=[("stg", 0), ("stg", 1)], reads=[("stgdma",)], dma_sem=stg_sem)
        for half in range(2):
            io = next_pso()

            def ft(pe, io=io, half=half):
                for cc in range(4):
                    c = half * 4 + cc
                    mm = pe.transpose(out=pso[io][:, cc * 128:cc * 128 + rows], in_=stg[0:rows, c * 128:(c + 1) * 128],
                                      identity=ident[0:rows, 0:rows])
                return mm
            P.pe_fence()
            P.op("pe", ft, reads=[("stg", 0), ("stg", 1), ("ident",)], writes=[("pso", io)])
            P.pe_fence()
            P.op("act", lambda a, io=io, half=half: a.activation(
                out=dst[:, half * 4:half * 4 + 4, :], in_=pso[io][:, :].rearrange("p (a b) -> p a b", a=4)[:, :, 0:rows], func=AF.Copy),
                reads=[("pso", io)], writes=[(dkey, half)])

    def conv_mixer(l, p, base_idx, finish, need):
        j = l // 2
        last_pass = (p == npass - 1)
        load_hist(sconv[j, p], 16, chist, "chist")
        mt_acquire()
        yield None
        for t in range(NTILE):
            if need is not None:
                yield ("need", need(t))
        kcol = lambda k, c: 192 + (j * 3 + k) * 8 + c
        for c in range(NCH):
            yield from slots_ready(base_idx + c + 1)
            s = (base_idx + c) % RING
            z = zf[c % 2]
            zs = z[:, 2 + NPR:ZFW].rearrange("p (s k) -> p s k", k=6)
            P.op("pool", lambda g_, z=z, c=c: g_.tensor_copy(out=z[:, 0:2], in_=convh[:, j, c, :]),
                 reads=[("convh", j)], writes=[("zf", c % 2, "h")])
            P.op("pool", lambda g_, zs=zs, c=c: g_.tensor_copy(out=zs[:, :, 0:2], in_=chist[:, c, :].rearrange("p (s k) -> p s k", k=2)),
                 reads=[("chist", c // 4)], writes=[("zf", c % 2, "hs")])
            for t in range(NTILE):
                cols = slice(t * TS, (t + 1) * TS)
                npr = TS if t < 2 else NPR - 2 * TS
                p0 = t * TS
                io_b = next_pso()
                i_c = next_psa()
                i_v = next_psa()
                for (bank, off) in ((pso[io_b], 0), (psa[i_c], 1024), (psa[i_v], 2048)):
                    def fm(pe, bank=bank, off=off, s=s, cols=cols):
                        for k in range(NCH):
                            mm = pe.matmul(bank[:, 0:TS], lhsT=ring[:, s, off + k * 128:off + (k + 1) * 128], rhs=hT[:, k, cols],
                                           start=(k == 0), stop=(k == NCH - 1))
                        return mm
                    wkey = ("pso", io_b) if off == 0 else (("psa", i_c) if off == 1024 else ("psa", i_v))
                    P.op("pe", fm, reads=[("h", k, t) for k in range(NCH)] + [("slot", s)], writes=[wkey])
                gi_ = (c * NTILE + t) % 2
                P.op("act", lambda a, gi_=gi_, i_c=i_c: a.activation(out=gcs[gi_], in_=psa[i_c][:, 0:TS], func=AF.Copy),
                     reads=[("psa", i_c)], writes=[("gcs", gi_)])
                P.op("dve", lambda v, z=z, gi_=gi_, i_v=i_v, p0=p0, npr=npr: v.tensor_tensor(
                    out=z[:, 2 + p0:2 + p0 + npr], in0=psa[i_v][:, 0:npr], in1=gcs[gi_][:, 0:npr], op=ALU.mult),
                    reads=[("psa", i_v), ("gcs", gi_)], writes=[("zf", c % 2, t)])
                k0, k1, k2 = kcol(0, c), kcol(1, c), kcol(2, c)
                P.op("act", lambda a, z=z, p0=p0, npr=npr, k0=k0: a.activation(out=tb[0][:, 0:npr], in_=z[:, p0:p0 + npr], func=AF.Copy,
                                                                             scale=params[:, k0:k0 + 1]),
                     reads=[("zf", c % 2, t), ("zf", c % 2, t - 1 if t > 0 else "h"), ("params",)], writes=[("tb", 0, "p")])
                P.op("dve", lambda v, z=z, p0=p0, npr=npr, k1=k1: v.scalar_tensor_tensor(
                    out=tb[1][:, 0:npr], in0=z[:, p0 + 1:p0 + 1 + npr], scalar=params[:, k1:k1 + 1], in1=tb[0][:, 0:npr],
                    op0=ALU.mult, op1=ALU.add),
                    reads=[("zf", c % 2, t), ("zf", c % 2, t - 1 if t > 0 else "h"), ("tb", 0, "p")], writes=[("tb", 1, "p")])
                P.op("dve", lambda v, z=z, p0=p0, npr=npr, k2=k2: v.scalar_tensor_tensor(
                    out=tb[2][:, 0:npr], in0=z[:, p0 + 2:p0 + 2 + npr], scalar=params[:, k2:k2 + 1], in1=tb[1][:, 0:npr],
                    op0=ALU.mult, op1=ALU.add),
                    reads=[("zf", c % 2, t), ("tb", 1, "p")], writes=[("tb", 2, "p")])
                P.op("dve", lambda v, io_b=io_b, p0=p0, npr=npr, c=c: v.tensor_tensor(
                    out=b2[:, c, p0:p0 + npr], in0=pso[io_b][:, 0:npr], in1=tb[2][:, 0:npr], op=ALU.mult),
                    reads=[("pso", io_b), ("tb", 2, "p")], writes=[("b2", c, t)])
                if t == NTILE - 1:
                    sl = slice(npr, TS)
                    P.op("dve", lambda v, zs=zs, gi_=gi_, i_v=i_v, sl=sl: v.tensor_tensor(
                        out=zs[:, :, 2:6], in0=v3(psa[i_v][:, sl]), in1=v3(gcs[gi_][:, sl]), op=ALU.mult),
                        reads=[("psa", i_v), ("gcs", gi_)], writes=[("zf", c % 2, "s")])
                    P.op("act", lambda a, zs=zs, k0=k0, sl=sl: a.activation(out=v3(tb[0][:, sl]), in_=zs[:, :, 0:4], func=AF.Copy,
                                                                    scale=params[:, k0:k0 + 1]),
                         reads=[("zf", c % 2, "s"), ("zf", c % 2, "hs"), ("params",)], writes=[("tb", 0, "s")])
                    P.op("dve", lambda v, zs=zs, k1=k1, sl=sl: v.scalar_tensor_tensor(
                        out=v3(tb[1][:, sl]), in0=zs[:, :, 1:5], scalar=params[:, k1:k1 + 1], in1=v3(tb[0][:, sl]),
                        op0=ALU.mult, op1=ALU.add),
                        reads=[("zf", c % 2, "s"), ("zf", c % 2, "hs"), ("tb", 0, "s")], writes=[("tb", 1, "s")])
                    P.op("dve", lambda v, zs=zs, k2=k2, sl=sl: v.scalar_tensor_tensor(
                        out=v3(tb[2][:, sl]), in0=zs[:, :, 2:6], scalar=params[:, k2:k2 + 1], in1=v3(tb[1][:, sl]),
                        op0=ALU.mult, op1=ALU.add),
                        reads=[("zf", c % 2, "s"), ("tb", 1, "s")], writes=[("tb", 2, "s")])
                    P.op("dve", lambda v, io_b=io_b, c=c, sl=sl: v.tensor_tensor(
                        out=b2[:, c, NPR:NTOK], in0=pso[io_b][:, sl], in1=tb[2][:, sl], op=ALU.mult),
                        reads=[("pso", io_b), ("tb", 2, "s")], writes=[("b2", c, t)])
                yield None
            P.op("pool", lambda g_, z=z, c=c: g_.tensor_copy(out=convh[:, j, c, :], in_=z[:, NPR:NPR + 2]),
                 reads=[("zf", c % 2, 2)], writes=[("convh", j)])
            P.op("pool", lambda g_, zs=zs, c=c: g_.tensor_copy(out=csamp[:, c, :].rearrange("p (s k) -> p s k", k=2), in_=zs[:, :, 4:6]),
                 reads=[("zf", c % 2, "s")], writes=[("csamp",)])
            release([base_idx + c])
        transposed_out(lambda c: csamp[:, c, :], 16, [(cs_s[j, p], 0, 16)], [("csamp",)])
        if last_pass:
            transposed_out(lambda c: convh[:, j, c, :], 2, [(cs_p[j], 0, 2)], [("convh", j)])
        yield from slots_ready(base_idx + 11)
        wslots = [(base_idx + 8 + i) % RING for i in range(3)]
        for t in range(NTILE):
            cols = slice(t * TS, (t + 1) * TS)
            bkeys = [("b2", cc, t) for cc in range(NCH)]
            for dc in range(NCH):
                io = next_pso()

                def fw(pe, dc=dc, io=io, cols=cols):
                    for cc in range(NCH):
                        mm = pe.matmul(pso[io][:, 0:TS], lhsT=ring[:, wslots[cc // 3], (cc % 3) * 1024 + dc * 128:(cc % 3) * 1024 + (dc + 1) * 128],
                                       rhs=b2[:, cc, cols], start=(cc == 0), stop=(cc == NCH - 1))
                    return mm
                P.op("pe", fw, reads=bkeys + [("slot", s_) for s_ in wslots], writes=[("pso", io)])
                P.op("act", lambda a, dc=dc, io=io, cols=cols: a.activation(out=acc[:, dc, cols], in_=pso[io][:, 0:TS], func=AF.Copy),
                     reads=[("pso", io)], writes=[("acc", dc, t)])
                yield None
            if t == NTILE - 1:
                release(range(base_idx + 8, base_idx + 11))
                mt_release()
            finish(t)

    def pool_mixer(l, p, base_idx, finish, need):
        j = l // 2
        last_pass = (p == npass - 1)
        s = base_idx % RING
        load_hist(spool[j, p].rearrange("s r d -> (s r) d"), NSS * 15, phist, "phist")
        P.op("sp", lambda h: h.dma_start(out=ps_s[j, p, :, 0:11, :], in_=spool[j, p, :, 4:15, :]), dma_sem=h2h_sem, final=True)
        mt_acquire()
        yield None
        yield from slots_ready(base_idx + 1)
        kc = {"k": 0}
        for t in range(NTILE):
            if need is not None:
                yield ("need", need(t))
            cols = slice(t * TS, (t + 1) * TS)
            p0 = t * TS
            npr = TS if t < 2 else NPR - 2 * TS
            smp = (t == NTILE - 1)
            W = 15 + npr + (NSS * 19 if smp else 0)
            if smp:
                ukeys = [("acc", c, 2) for c in range(NCH)]
                transposed_out(lambda c: acc[:, c, NPR:NTOK], NSM, [(ps_s[j, p, s_, 11:15, :], 4 * s_, 4) for s_ in range(NSS)], ukeys)
                yield None
            for g in range(4):
                w = 2 << g
                for c in (2 * g, 2 * g + 1):
                    k = kc["k"] % 2
                    kc["k"] += 1
                    bu = uft[k]
                    eng = "pool" if c % 2 == 0 else "dve"
                    P.op("pool", lambda g_, bu=bu, c=c: g_.tensor_copy(out=bu[:, 0:15], in_=poolh[:, j, c, :]),
                         reads=[("poolh", j, c)], writes=[("uft", k, "h")])
                    P.op(eng, lambda v, bu=bu, c=c, p0=p0, npr=npr: v.tensor_copy(out=bu[:, 15:15 + npr], in_=acc[:, c, p0:p0 + npr]),
                         reads=[("acc", c, t)], writes=[("uft", k, "u")])
                    ufk = [("uft", k, "h"), ("uft", k, "u")]
                    if smp:
                        us = bu[:, 15 + npr:W].rearrange("p (s k) -> p s k", k=19)
                        P.op("pool", lambda g_, us=us, c=c: g_.tensor_copy(out=us[:, :, 0:15], in_=phist[:, c, :].rearrange("p (s k) -> p s k", k=15)),
                             reads=[("phist", c // 4)], writes=[("uft", k, "hs")])
                        P.op("pool", lambda g_, us=us, c=c: g_.tensor_copy(out=us[:, :, 15:19], in_=acc[:, c, NPR:NTOK].rearrange("p (s k) -> p s k", k=4)),
                             reads=[("acc", c, 2)], writes=[("uft", k, "us")])
                        ufk = ufk + [("uft", k, "hs"), ("uft", k, "us")]
                    src, skeys = bu, ufk
                    sh = 1
                    bi = 0
                    while sh < w:
                        dst = wbuf[eng][bi]
                        dk = ("wbuf", eng, bi)
                        lo = 2 * sh - 1
                        P.op(eng, lambda v, dst=dst, src=src, lo=lo, sh=sh, W=W: v.tensor_tensor(
                            out=dst[:, lo:W], in0=src[:, lo:W], in1=src[:, lo - sh:W - sh], op=ALU.add),
                            reads=skeys, writes=[dk])
                        src, skeys = dst, [dk]
                        sh *= 2
                        bi ^= 1
                    win = src
                    P.op("act", lambda a, bu=bu, c=c, npr=npr: a.activation(out=poolh[:, j, c, :], in_=bu[:, npr:npr + 15], func=AF.Copy),
                         reads=ufk, writes=[("poolh", j, c)])
                    P.op("dve", lambda v, win=win, bu=bu, c=c, w=w, p0=p0, npr=npr: v.scalar_tensor_tensor(
                        out=b2[:, c, p0:p0 + npr], in0=win[:, 15:15 + npr], scalar=1.0 / w, in1=bu[:, 15:15 + npr],
                        op0=ALU.mult, op1=ALU.subtract),
                        reads=skeys + ufk, writes=[("b2", c, t)])
                    if smp:
                        wins = win[:, 15 + npr:W].rearrange("p (s k) -> p s k", k=19)
                        P.op("dve", lambda v, wins=wins, us=us, c=c, w=w: v.scalar_tensor_tensor(
                            out=b2[:, c, NPR:NTOK].rearrange("p (s k) -> p s k", k=4), in0=wins[:, :, 15:19], scalar=1.0 / w, in1=us[:, :, 15:19],
                            op0=ALU.mult, op1=ALU.subtract),
                            reads=skeys + ufk, writes=[("b2", c, t)])
                    if p == 0 and t == 0:
                        P.op("dve", lambda v, win=win, g=g: v.tensor_tensor(out=tmpf[:, 0:15], in0=win[:, 15:30], in1=invc[:, g, 0:15], op=ALU.mult),
                             reads=skeys + [("invc", g)], writes=[("tmpf",)])
                        P.op("dve", lambda v, bu=bu, c=c: v.tensor_tensor(out=b2[:, c, 0:15], in0=tmpf[:, 0:15], in1=bu[:, 15:30], op=ALU.subtract),
                             reads=[("tmpf",)] + ufk, writes=[("b2", c, t)])
                    yield None
                for dsub in range(2):
                    io = next_pso()
                    dc = 2 * g + dsub

                    def fp(pe, io=io, g=g, dsub=dsub, cols=cols):
                        for cc in range(2):
                            o_ = (g * 2 + cc) * 256 + dsub * 128
                            mm = pe.matmul(pso[io][:, 0:TS], lhsT=ring[:, s, o_:o_ + 128], rhs=b2[:, 2 * g + cc, cols],
                                           start=(cc == 0), stop=(cc == 1))
                        return mm
                    P.op("pe", fp, reads=[("b2", 2 * g, t), ("b2", 2 * g + 1, t), ("slot", s)], writes=[("pso", io)])
                    sc = 240 + j * 8 + dc
                    P.op("dve", lambda v, io=io, dc=dc, sc=sc, cols=cols: v.tensor_scalar(out=acc[:, dc, cols], in0=pso[io][:, 0:TS],
                                                                                         scalar1=params[:, sc:sc + 1], scalar2=None, op0=ALU.mult),
                         reads=[("pso", io), ("params",)], writes=[("acc", dc, t)])
                yield None
            if t == NTILE - 1:
                release([base_idx])
                if last_pass:
                    transposed_out(lambda c: poolh[:, j, c, :], 15, [(ps_p[j], 0, 15)], [("poolh", j, c) for c in range(NCH)])
                mt_release()
            finish(t)

    def run_pipeline(gens, kinds):
        n = len(gens)
        waiting = [None] * n
        finished = [False] * n

        def advance(i, primary):
            w_ = waiting[i]
            if w_ is not None:
                if w_[0] == "need":
                    if primary:
                        P.flush_through(w_[1])
                    elif w_[1] not in P.done:
                        return
                elif w_[0] == "slot":
                    if ws["issued"] < min(w_[1], ws["total"]):
                        assert not primary, ("primary blocked on slot", w_, ws)
                        return
                waiting[i] = None
            try:
                r = next(gens[i])
            except StopIteration:
                finished[i] = True
                return
            if r is not None:
                waiting[i] = r

        cur = 0
        while cur < n:
            advance(cur, True)
            if finished[cur]:
                cur += 1
                continue
            if cfg.get("interleave", True) and cur + 1 < n and not (kinds[cur] == "ffn" and kinds[cur + 1] == "ffn"):
                advance(cur + 1, False)
                assert not finished[cur + 1]
            P.drain_step()

    for p in range(npass):
        if p == 0:
            boundary(None, 0)
            ws["extra"] = [("xs", i) for i in range(3)]
            issue_slot_loads(RING)
            ws["extra"] = []
        idx = p * SPP
        subs = []
        for l in range(n_layers):
            subs.append(("ffn", l, 0))
            subs.append(("mix", l, 0))
            subs.append(("ffn", l, 1))
        subs = subs[:cfg.get("nsub", 12)]

        def pre_of(si, t):
            kind, l, s_ = subs[si]
            if kind == "ffn":
                prenorm(l, 0 if s_ == 0 else 4, t)
            else:
                prenorm(l, 2, t, to_u=(l % 2 == 1))

        for t in range(NTILE):
            if subs:
                pre_of(0, t)
        gens, kinds = [], []
        for si, (kind, l, s_) in enumerate(subs):
            def finish(t, si=si, kind=kind, l=l, s_=s_):
                P.defer = True
                if kind == "ffn":
                    postnorm(l, 1 if s_ == 0 else 5, t, half=True)
                else:
                    postnorm(l, 3, t, half=False)
                if si + 1 < len(subs):
                    pre_of(si + 1, t)
                P.defer = False
                P.mark(("chain", p, si, t))
            need = (lambda t, si=si: ("chain", p, si - 1, t)) if si > 0 else None
            if kind == "ffn":
                prev = kinds[-1] if si > 0 else None
                if prev == "conv":
                    gens.append(ffn(idx, finish, need, groups=[5, 2, 5, 5, 5], prefix=2))
                else:
                    gens.append(ffn(idx, finish, need, groups=[5, 4, 4, 4, 5], prefix=2))
                kinds.append("ffn")
                idx += NFC
            elif l % 2 == 0:
                gens.append(conv_mixer(l, p, idx, finish, need))
                kinds.append("conv")
                idx += 11
            else:
                gens.append(pool_mixer(l, p, idx, finish, need))
                kinds.append("pool")
                idx += 1
        run_pipeline(gens, kinds)
        P.flush_all()
        boundary(p, p + 1 if p + 1 < npass else None)

    P.emit()
    return nc


def _kblock(w):
    return w.reshape(8, 128, w.shape[1]).transpose(1, 0, 2).reshape(128, 8 * w.shape[1])


def build_wstream(ffn_w_gate, ffn_w_up, ffn_w_down, conv_w_in, conv_w_out, pool_w_group):
    ws = np.zeros((SLOTS_FULL, 128, SLOTW), np.float32)
    i = 0
    for l in range(4):
        for s in range(2):
            if s == 1:
                j = l // 2
                if l % 2 == 0:
                    for c in range(8):
                        for part in range(3):
                            ws[i, :, part * 1024:(part + 1) * 1024] = _kblock(conv_w_in[j][:, part * 1024 + c * 128:part * 1024 + (c + 1) * 128])
                        i += 1
                    wo = conv_w_out[j].reshape(8, 128, 1024).transpose(1, 0, 2).reshape(128, 8192)
                    ws[i, :, :] = wo[:, 0:3072]
                    ws[i + 1, :, :] = wo[:, 3072:6144]
                    ws[i + 2, :, 0:2048] = wo[:, 6144:8192]
                    i += 3
                else:
                    ws[i, :, 0:2048] = pool_w_group[j].reshape(4, 2, 128, 256).transpose(2, 0, 1, 3).reshape(128, 2048)
                    i += 1
            for jf in range(NFC):
                ws[i, :, 0:1024] = _kblock(ffn_w_gate[l, s][:, jf * 128:(jf + 1) * 128])
                ws[i, :, 1024:2048] = _kblock(ffn_w_up[l, s][:, jf * 128:(jf + 1) * 128])
                ws[i, :, 2048:3072] = ffn_w_down[l, s][jf * 128:(jf + 1) * 128, :]
                i += 1
    assert i == SLOTS_FULL
    return ws


def build_params(norm_gains, conv_kernel, pool_scale):
    pr = np.zeros((128, 256), np.float32)
    pr[:, 0:192] = norm_gains.reshape(24, 8, 128).transpose(2, 0, 1).reshape(128, 192)
    pr[:, 192:240] = conv_kernel.reshape(6, 8, 128).transpose(2, 0, 1).reshape(128, 48)
    pr[:, 240:256] = pool_scale.reshape(2, 8, 128).transpose(2, 0, 1).reshape(128, 16)
    return pr


_NC_CACHE = {}


def kernel(x_prompt, x_sample, state_conv, state_pool, norm_gains, ffn_w_gate, ffn_w_up, ffn_w_down,
           conv_w_in, conv_kernel, conv_w_out, pool_w_group, pool_scale):
    f = lambda a: np.asarray(a, dtype=np.float32)
    x_prompt, x_sample, state_conv, state_pool = f(x_prompt), f(x_sample), f(state_conv), f(state_pool)
    ws = build_wstream(f(ffn_w_gate), f(ffn_w_up), f(ffn_w_down), f(conv_w_in), f(conv_w_out), f(pool_w_group))
    pr = build_params(f(norm_gains), f(conv_kernel), f(pool_scale))
    ident = np.eye(128, dtype=np.float32)
    ws_used = np.ascontiguousarray(ws[:slots_per_pass(CFG['n_layers'])])
    in_maps = []
    for i in range(N_CORES):
        xin = np.empty((NPASS, NTOK, D), np.float32)
        sc = np.empty((2, NPASS, 16, D), np.float32)
        sp = np.empty((2, NPASS, NSS, 15, D), np.float32)
        for p in range(NPASS):
            xin[p, 0:NPR] = x_prompt[i, p * NPR:(p + 1) * NPR]
            b0 = i * 16 + p * NSS
            xin[p, NPR:] = x_sample[b0:b0 + NSS].reshape(NSM, D)
            sc[:, p] = state_conv[:, b0:b0 + NSS].reshape(2, 16, D)
            sp[:, p] = state_pool[:, b0:b0 + NSS]
        in_maps.append({"x_in": xin, "sconv": sc, "spool": sp, "wstream": ws_used, "params": pr, "ident": ident})
    key = (CFG["n_layers"], CFG["npass"], CFG.get("nsub", 12))
    if key not in _NC_CACHE:
        _NC_CACHE[key] = build_program(CFG)
    nc = _NC_CACHE[key]
    res = run_bass_kernel_spmd(nc, in_maps, core_ids=list(range(N_CORES)))
    R = res.results
    y_prompt = np.empty((8, 2048, D), np.float32)
    y_sample = np.empty((128, 4, D), np.float32)
    ncp = np.empty((2, 8, 2, D), np.float32)
    ncs = np.empty((2, 128, 2, D), np.float32)
    npp = np.empty((2, 8, 15, D), np.float32)
    nps = np.empty((2, 128, 15, D), np.float32)
    for i in range(N_CORES):
        r = R[i]
        for p in range(NPASS):
            b0 = i * 16 + p * NSS
            y_prompt[i, p * NPR:(p + 1) * NPR] = r["y_o"][p, 0:NPR]
            y_sample[b0:b0 + NSS] = r["y_o"][p, NPR:].reshape(NSS, 4, D)
            ncs[:, b0:b0 + NSS] = r["cs_s"][:, p].reshape(2, NSS, 2, D)
            nps[:, b0:b0 + NSS] = r["ps_s"][:, p]
        ncp[:, i] = r["cs_p"]
        npp[:, i] = r["ps_p"]
    return (y_prompt, y_sample, ncp, ncs, npp, nps)
```

```python
import collections
import numpy as np
import concourse.bass as bass
import concourse.mybir as mybir
from concourse.bass_utils import run_bass_kernel_spmd

F32 = mybir.dt.float32
BF16 = mybir.dt.bfloat16
AF = mybir.ActivationFunctionType
ALU = mybir.AluOpType

D = 1024
DFF = 2816
NCH = 8
NFC = 22
NPR = 1024
NSS = 8
NSM = 32
NTOK = NPR + NSM
TS = 352
NTILE = 3
NPASS = 2
GROUPS = [5, 5, 4, 4, 4]
GMAX = 5
RING = 10
SLOTW = 3072
EPS = 1e-6
N_CORES = 8
SLOTS_FULL = 4 * 44 + 2 * 11 + 2 * 1


def slots_per_pass(n_layers):
    return sum(44 + (11 if l % 2 == 0 else 1) for l in range(n_layers))

CFG = {"n_layers": 4, "npass": 2, "nsub": 12, "interleave": True}


class Prog:
    def __init__(self, nc):
        self.nc = nc
        self.eng = {}
        for name in ("pe", "act", "dve", "pool", "sp"):
            self.eng[name] = dict(sem=nc.alloc_semaphore("s_" + name), count=0, waited={}, ops=[])
        self.lastw = {}
        self.readers = {}
        self.dma_cnt = {}
        self.final_tokens = {}
        self.defer = False
        self.pending = collections.deque()
        self.done = set()
        self.fence_next = False

    def pe_fence(self):
        self.fence_next = True

    def drain_step(self, nbulk=4):
        n = 0
        while self.pending:
            item = self.pending[0]
            if item[0] == "M":
                self.pending.popleft()
                self.done.add(item[1])
                continue
            if item[0] == "L":
                if n > 0:
                    break
                self.pending.popleft()
                self._op(*item[1], **item[2])
                n = nbulk - 2
                continue
            self.pending.popleft()
            self._op(*item[1], **item[2])
            n += 1
            if n >= nbulk:
                break

    def flush_through(self, marker):
        if marker in self.done:
            return
        assert any(it[0] == "M" and it[1] == marker for it in self.pending), marker
        while self.pending:
            item = self.pending.popleft()
            if item[0] == "M":
                self.done.add(item[1])
                if item[1] == marker:
                    return
                continue
            self._op(*item[1], **item[2])

    def flush_all(self):
        while self.pending:
            item = self.pending.popleft()
            if item[0] != "M":
                self._op(*item[1], **item[2])
            else:
                self.done.add(item[1])

    def mark(self, marker):
        self.pending.append(("M", marker))

    def op(self, eng, fn, reads=(), writes=(), dma_sem=None, final=False, tag="B"):
        if self.defer:
            self.pending.append((tag, (eng, fn), dict(reads=list(reads), writes=list(writes), dma_sem=dma_sem, final=final)))
            return None
        return self._op(eng, fn, reads, writes, dma_sem, final)

    def dma_sem(self, name):
        s = self.nc.alloc_semaphore(name)
        self.dma_cnt[s.num] = 0
        return s

    def _op(self, eng, fn, reads=(), writes=(), dma_sem=None, final=False):
        e = self.eng[eng]
        deps = {}

        def add(tok):
            if tok is None:
                return
            k = tok[0].num
            if k not in deps or deps[k][1] < tok[1]:
                deps[k] = tok

        for r in reads:
            add(self.lastw.get(r))
        for w in writes:
            add(self.lastw.get(w))
            for tok in self.readers.get(w, {}).values():
                add(tok)
        waits = []
        for k, (sem, val) in deps.items():
            if eng == "pe" and sem is e["sem"]:
                continue
            if e["waited"].get(k, 0) >= val:
                continue
            e["waited"][k] = val
            waits.append((sem, val))
        if eng == "pe" and self.fence_next:
            self.fence_next = False
            if e["count"] > 0:
                waits.append((e["sem"], e["count"]))
        if dma_sem is not None:
            self.dma_cnt[dma_sem.num] += 16
            tok = (dma_sem, self.dma_cnt[dma_sem.num])
            inc = (dma_sem, 16)
        else:
            e["count"] += 1
            tok = (e["sem"], e["count"])
            inc = (e["sem"], 1)
        e["ops"].append((waits, fn, inc))
        for r in reads:
            self.readers.setdefault(r, {})[tok[0].num] = tok
        for w in writes:
            self.lastw[w] = tok
            self.readers[w] = {}
        if final:
            self.final_tokens[tok[0].num] = tok
        return tok

    def emit(self):
        nc = self.nc
        finals = list(self.final_tokens.values())

        class FirstWait:
            def __init__(self, h):
                self._h = h
                self._pend = None

            def __getattr__(self, name):
                attr = getattr(self._h, name)
                if not callable(attr):
                    return attr

                def g(*a, **k):
                    ins = attr(*a, **k)
                    if self._pend is not None and hasattr(ins, "_wait_ge"):
                        s_, v_ = self._pend
                        self._pend = None
                        ins._wait_ge(s_, v_)
                    return ins
                return g

        def replay(name, h):
            px = FirstWait(h)
            for waits, fn, inc in self.eng[name]["ops"]:
                for s, v in waits[:-1]:
                    h.wait_ge(s, v)
                px._pend = waits[-1] if waits else None
                ins = fn(px)
                assert px._pend is None
                ins.then_inc(inc[0], inc[1])

        with nc.Block() as block:
            @block.tensor
            def _(h):
                replay("pe", h)

            @block.scalar
            def _(h):
                replay("act", h)

            @block.vector
            def _(h):
                replay("dve", h)

            @block.gpsimd
            def _(h):
                replay("pool", h)

            @block.sync
            def _(h):
                replay("sp", h)
                for s, v in finals:
                    h.wait_ge(s, v)


def v3(ap):
    return ap.rearrange("p (s k) -> p s k", k=4)


def tiles_of(c0, c1):
    return [t for t in range(NTILE) if c0 < (t + 1) * TS and c1 > t * TS]


def build_program(cfg):
    n_layers = cfg["n_layers"]
    npass = cfg["npass"]
    nc = bass.Bass("TRN2", target_bir_lowering=False)
    P = Prog(nc)
    SLOTS_PER_PASS = slots_per_pass(n_layers)
    _subs = []
    for l in range(n_layers):
        _subs += [22, 11 if l % 2 == 0 else 1, 22]
    SPP = max(1, sum(_subs[:cfg.get("nsub", 12)]))

    x_in = nc.dram_tensor("x_in", [NPASS, NTOK, D], F32, kind="ExternalInput").ap()
    sconv = nc.dram_tensor("sconv", [2, NPASS, 16, D], F32, kind="ExternalInput").ap()
    spool = nc.dram_tensor("spool", [2, NPASS, NSS, 15, D], F32, kind="ExternalInput").ap()
    wstream = nc.dram_tensor("wstream", [SLOTS_PER_PASS, 128, SLOTW], F32, kind="ExternalInput").ap()
    params_d = nc.dram_tensor("params", [128, 256], F32, kind="ExternalInput").ap()
    ident_d = nc.dram_tensor("ident", [128, 128], F32, kind="ExternalInput").ap()
    y_o = nc.dram_tensor("y_o", [NPASS, NTOK, D], F32, kind="ExternalOutput").ap()
    cs_p = nc.dram_tensor("cs_p", [2, 2, D], F32, kind="ExternalOutput").ap()
    cs_s = nc.dram_tensor("cs_s", [2, NPASS, 16, D], F32, kind="ExternalOutput").ap()
    ps_p = nc.dram_tensor("ps_p", [2, 15, D], F32, kind="ExternalOutput").ap()
    ps_s = nc.dram_tensor("ps_s", [2, NPASS, NSS, 15, D], F32, kind="ExternalOutput").ap()

    xT = nc.alloc_sbuf_tensor("xT", [128, NCH, NTOK], F32)
    hreg = nc.alloc_sbuf_tensor("hreg", [128, NCH * NTOK // 2], F32)
    hT = hreg.bitcast(BF16).reshape([128, NCH, NTOK])
    xstg = [hreg[:, i * 1024:(i + 1) * 1024] for i in range(4)]
    acc = nc.alloc_sbuf_tensor("acc", [128, NCH, NTOK], F32)
    b2 = nc.alloc_sbuf_tensor("b2", [128, NCH, NTOK], BF16)
    ring = nc.alloc_sbuf_tensor("ring", [128, RING, SLOTW], BF16)
    MTW = 5400
    mt = nc.alloc_sbuf_tensor("mt", [128, MTW], F32)
    sa = nc.alloc_sbuf_tensor("sa", [128, 2, TS], F32)
    gt = nc.alloc_sbuf_tensor("gt", [128, 2, GMAX, TS], BF16)
    params = nc.alloc_sbuf_tensor("params_sb", [128, 256], F32)
    params_h = nc.alloc_sbuf_tensor("params_h", [128, 256], F32)
    ident = nc.alloc_sbuf_tensor("ident_sb", [128, 128], F32)
    ones_bf = nc.alloc_sbuf_tensor("ones_bf", [128, 128], BF16)
    epst = nc.alloc_sbuf_tensor("epst", [128, 1], F32)
    dmy = nc.alloc_sbuf_tensor("dmy", [128, 8], F32)
    invc = nc.alloc_sbuf_tensor("invc", [128, 4, 16], F32)
    tmp = nc.alloc_sbuf_tensor("tmp", [128, 2, TS], F32)
    stg = nc.alloc_sbuf_tensor("stg", [128, D], F32)
    phist = nc.alloc_sbuf_tensor("phist", [128, NCH, NSS * 15], F32)
    chist = nc.alloc_sbuf_tensor("chist", [128, NCH, 16], F32)
    csamp = nc.alloc_sbuf_tensor("csamp", [128, NCH, 16], F32)
    convh = nc.alloc_sbuf_tensor("convh", [128, 2, NCH, 2], F32)
    poolh = nc.alloc_sbuf_tensor("poolh", [128, 2, NCH, 15], F32)

    ZFW = 2 + NPR + NSS * 6
    zf = [mt[:, i * ZFW:(i + 1) * ZFW] for i in range(2)]
    o = 2 * ZFW
    gcs = [mt[:, o + i * TS:o + (i + 1) * TS] for i in range(2)]
    o += 2 * TS
    tb = [mt[:, o + i * TS:o + (i + 1) * TS] for i in range(3)]
    o += 3 * TS
    assert o <= MTW
    UFW = 15 + NPR + NSS * 19
    uf = [mt[:, i * UFW:(i + 1) * UFW] for i in range(2)]
    wa = mt[:, 2 * UFW:3 * UFW]
    wb = mt[:, 3 * UFW:4 * UFW]
    UTW = 15 + TS + NSS * 19
    uft = [mt[:, i * UTW:(i + 1) * UTW] for i in range(2)]
    wbuf = {"pool": [mt[:, (2 + i) * UTW:(3 + i) * UTW] for i in range(2)],
            "dve": [mt[:, (4 + i) * UTW:(5 + i) * UTW] for i in range(2)]}
    tmpf = mt[:, 6 * UTW:6 * UTW + 16]
    assert 6 * UTW + 16 <= MTW

    psa = [nc.alloc_psum_tensor(f"psa{i}", [128, 512], F32) for i in range(4)]
    pso = [nc.alloc_psum_tensor(f"pso{i}", [128, 512], F32) for i in range(3)]
    pss = nc.alloc_psum_tensor("pss", [128, 512], F32)

    ring_sem = [P.dma_sem(f"ring{i}") for i in range(RING)]
    xstg_sem = [P.dma_sem(f"xstg{i}") for i in range(4)]
    stg_sem = P.dma_sem("stg")
    misc_sem = P.dma_sem("misc")
    misc2_sem = P.dma_sem("misc2")
    h2h_sem = P.dma_sem("h2h")

    cnt = {"psa": 0, "pso": 0, "tmp": 0, "xstg": 0}

    def next_psa():
        i = cnt["psa"] % 4
        cnt["psa"] += 1
        return i

    def next_pso():
        i = cnt["pso"] % 3
        cnt["pso"] += 1
        return i

    HKEYS = [("h", c, t) for c in range(NCH) for t in range(NTILE)]

    ws = {"issued": 0, "total": npass * SPP, "rel": set(), "ptr": 0}

    def issue_slot_loads(upto):
        while ws["issued"] < min(upto, ws["total"]):
            i = ws["issued"]
            s = i % RING
            src = wstream[i % SPP]
            P.op("pool", lambda g, s=s, src=src: g.dma_start(out=ring[:, s, :], in_=src),
                 reads=list(ws.get("extra", [])), writes=[("slot", s)], dma_sem=ring_sem[s])
            ws["issued"] += 1

    def release(idxs):
        ws["rel"].update(idxs)
        while ws["ptr"] in ws["rel"]:
            ws["rel"].discard(ws["ptr"])
            ws["ptr"] += 1
        issue_slot_loads(ws["ptr"] + RING)

    def slots_ready(upto):
        while ws["issued"] < min(upto, ws["total"]):
            yield ("slot", upto)

    P.op("sp", lambda h: h.dma_start(out=params[:], in_=params_d), writes=[("params",)], dma_sem=misc_sem)
    P.op("sp", lambda h: h.dma_start(out=ident[:], in_=ident_d), writes=[("ident",)], dma_sem=misc2_sem)
    P.op("dve", lambda v: v.memset(ones_bf[:], 1.0), writes=[("ones",)])
    P.op("dve", lambda v: v.memset(epst[:], EPS), writes=[("eps",)])
    P.op("dve", lambda v: v.memset(dmy[:], 1.0), writes=[("dmy",)])
    P.op("dve", lambda v: v.tensor_scalar(out=params_h[:], in0=params[:], scalar1=0.5, scalar2=None, op0=ALU.mult),
         reads=[("params",)], writes=[("params_h",)])
    for g in range(4):
        w = 2 << g
        P.op("dve", lambda v, g=g, w=w: v.memset(invc[:, g, :], 1.0 / w), writes=[("invc", g)])
        for i in range(min(w - 1, 15)):
            P.op("dve", lambda v, g=g, i=i: v.memset(invc[:, g, i:i + 1], 1.0 / (i + 1)), writes=[("invc", g)])
    P.op("dve", lambda v: v.memset(convh[:], 0.0), writes=[("convh", 0), ("convh", 1)])
    P.op("dve", lambda v: v.memset(poolh[:], 0.0), writes=[("poolh", j_, c_) for j_ in range(2) for c_ in range(NCH)])
    issue_slot_loads(2)

    def gcol(l, n, c):
        return (l * 6 + n) * 8 + c

    def mt_acquire():
        keys = [("xs", i_) for i_ in range(5)] + [("xs", i_, h_) for i_ in (3, 4) for h_ in (0, 1)]
        P.op("dve", lambda v: v.memset(dmy[:, 2:3], 0.0), reads=[("dmy",)], writes=keys)
        P.op("pool", lambda g_: g_.memset(dmy[:, 3:4], 0.0), reads=[("dmy",)], writes=keys)
        P.op("act", lambda a: a.activation(out=dmy[:, 4:5], in_=dmy[:, 6:7], func=AF.Copy), reads=[("dmy",)], writes=keys)

    def mt_release():
        P.op("dve", lambda v: v.memset(dmy[:, 2:3], 0.0), reads=[("dmy",)], writes=[("mtfree", "dve")])
        P.op("pool", lambda g_: g_.memset(dmy[:, 3:4], 0.0), reads=[("dmy",)], writes=[("mtfree", "pool")])
        P.op("act", lambda a: a.activation(out=dmy[:, 4:5], in_=dmy[:, 6:7], func=AF.Copy), reads=[("dmy",)], writes=[("mtfree", "act")])

    def norm_stats(src, skey, t):
        cols = slice(t * TS, (t + 1) * TS)
        for c in range(NCH):
            P.op("act", lambda a, c=c: a.activation(out=b2[:, c, cols], in_=src[:, c, cols], func=AF.Square),
                 reads=[(skey, c, t)], writes=[("b2", c, t)])

        def f(pe):
            for c in range(NCH):
                mm = pe.matmul(pss[:, 0:TS], lhsT=ones_bf[:], rhs=b2[:, c, cols], start=(c == 0), stop=(c == NCH - 1))
            return mm
        P.op("pe", f, reads=[("b2", c, t) for c in range(NCH)] + [("ones",)], writes=[("pss",)], tag="L")

        P.op("act", lambda a: a.activation(out=dmy[:, 0:1], in_=dmy[:, 6:7], func=AF.Ln), reads=[("dmy",)], writes=[("dmy0",)])
        P.op("act", lambda a: a.activation(out=pss[:, 0:TS], in_=pss[:, 0:TS], func=AF.Ln, scale=1.0 / D, bias=epst[:, 0:1]),
             reads=[("pss",), ("eps",)], writes=[("pss",)], tag="L")

        def fe(a):
            ins = a.activation(out=pss[:, 0:TS], in_=pss[:, 0:TS], func=AF.Exp, scale=-0.5)
            return ins
        P.op("act", fe, reads=[("pss",)], writes=[("pss",)], tag="L")

    def prenorm(l, n, t, to_u=False):
        cols = slice(t * TS, (t + 1) * TS)
        norm_stats(xT, "x", t)
        for c in range(NCH):
            dst = acc if to_u else hT
            dkey = "acc" if to_u else "h"
            gc_ = gcol(l, n, c)
            P.op("dve", lambda v, c=c, dst=dst, gc_=gc_: v.scalar_tensor_tensor(
                out=dst[:, c, cols], in0=xT[:, c, cols], scalar=params[:, gc_:gc_ + 1], in1=pss[:, 0:TS],
                op0=ALU.mult, op1=ALU.mult),
                reads=[("x", c, t), ("pss",), ("params",)], writes=[(dkey, c, t)])

    def postnorm(l, n, t, half):
        cols = slice(t * TS, (t + 1) * TS)
        norm_stats(acc, "acc", t)
        pt = params_h if half else params
        pk = ("params_h",) if half else ("params",)
        for c in range(NCH):
            gc_ = gcol(l, n, c)
            P.op("dve", lambda v, c=c, gc_=gc_: v.scalar_tensor_tensor(
                out=acc[:, c, cols], in0=acc[:, c, cols], scalar=pt[:, gc_:gc_ + 1], in1=pss[:, 0:TS],
                op0=ALU.mult, op1=ALU.mult),
                reads=[("acc", c, t), ("pss",), pk], writes=[("acc", c, t)])
            if c < 5:
                P.op("pool", lambda g_, c=c: g_.tensor_tensor(out=xT[:, c, cols], in0=xT[:, c, cols], in1=acc[:, c, cols], op=ALU.add),
                     reads=[("x", c, t), ("acc", c, t)], writes=[("x", c, t)])
        for c in range(5, NCH):
            P.op("dve", lambda v, c=c: v.tensor_tensor(out=xT[:, c, cols], in0=xT[:, c, cols], in1=acc[:, c, cols], op=ALU.add),
                 reads=[("x", c, t), ("acc", c, t)], writes=[("x", c, t)])

    def ffn(base_idx, finish, need, groups=GROUPS, prefix=1):
        gstart = [0]
        for gsz in groups:
            gstart.append(gstart[-1] + gsz)
        assert gstart[-1] == NFC
        units = [(gi, t) for t in range(NTILE) for gi in range(prefix)]
        units += [(gi, t) for gi in range(prefix, len(groups)) for t in range(NTILE)]
        qc = {"q": 0}

        def AB(ui):
            gi, t = units[ui]
            u = ui % 2
            cols = slice(t * TS, (t + 1) * TS)
            if gi == 0 and need is not None:
                yield ("need", need(t))
            yield from slots_ready(base_idx + gstart[gi + 1])
            for jj in range(groups[gi]):
                j = gstart[gi] + jj
                s = (base_idx + j) % RING
                q = qc["q"]
                qc["q"] += 1
                ia, ib = 2 * (q % 2), 2 * (q % 2) + 1

                def fa(pe, s=s, ia=ia, off=0, cols=cols):
                    for k in range(NCH):
                        mm = pe.matmul(psa[ia][:, 0:TS], lhsT=ring[:, s, off + k * 128:off + (k + 1) * 128],
                                       rhs=hT[:, k, cols], start=(k == 0), stop=(k == NCH - 1))
                    return mm
                P.op("pe", fa, reads=[("h", k, t) for k in range(NCH)] + [("slot", s)], writes=[("psa", ia)])
                P.op("pe", lambda pe, s=s, ib=ib, fa=fa: fa(pe, s, ib, 1024),
                     reads=[("h", k, t) for k in range(NCH)] + [("slot", s)], writes=[("psa", ib)])
                P.op("act", lambda a, ia=ia, q=q: a.activation(out=sa[:, q % 2, :], in_=psa[ia][:, 0:TS], func=AF.Silu),
                     reads=[("psa", ia)], writes=[("sa", q % 2)])
                P.op("dve", lambda v, ib=ib, q=q, u=u, jj=jj: v.tensor_tensor(
                    out=gt[:, u, jj, :], in0=psa[ib][:, 0:TS], in1=sa[:, q % 2, :], op=ALU.mult),
                    reads=[("psa", ib), ("sa", q % 2)], writes=[("g", u, jj)])
                yield None

        def OUT(ui):
            gi, t = units[ui]
            u = ui % 2
            cols = slice(t * TS, (t + 1) * TS)
            G = groups[gi]
            slots = [(base_idx + gstart[gi] + jj) % RING for jj in range(G)]
            for c in range(NCH):
                io = next_pso()

                def fo(pe, c=c, io=io, G=G, slots=slots, u=u):
                    for jj in range(G):
                        mm = pe.matmul(pso[io][:, 0:TS], lhsT=ring[:, slots[jj], 2048 + c * 128:2048 + (c + 1) * 128],
                                       rhs=gt[:, u, jj, :], start=(jj == 0), stop=(jj == G - 1))
                    return mm
                P.op("pe", fo, reads=[("g", u, jj) for jj in range(G)] + [("slot", s) for s in slots], writes=[("pso", io)])
                if gi == 0:
                    P.op("act", lambda a, c=c, io=io, cols=cols: a.activation(out=acc[:, c, cols], in_=pso[io][:, 0:TS], func=AF.Copy),
                         reads=[("pso", io)], writes=[("acc", c, t)])
                else:
                    P.op("dve", lambda v, c=c, io=io, cols=cols: v.tensor_tensor(out=acc[:, c, cols], in0=pso[io][:, 0:TS], in1=acc[:, c, cols], op=ALU.add),
                         reads=[("pso", io), ("acc", c, t)], writes=[("acc", c, t)])
                yield None
            if t == NTILE - 1:
                release(range(base_idx + gstart[gi], base_idx + gstart[gi + 1]))
            if gi == len(groups) - 1:
                finish(t)

        yield from AB(0)
        for ui in range(1, len(units)):
            yield from AB(ui)
            yield from OUT(ui - 1)
        yield from OUT(len(units) - 1)

    RB = [(rb * 128, 128) for rb in range(8)] + [(1024, 32)]
    xs = [mt[:, i * 1024:(i + 1) * 1024] for i in range(5)]
    xs_sem = [P.dma_sem(f"xs{i}") for i in range(5)]
    GUARD = [("mtfree", e_) for e_ in ("dve", "pool", "act")]
    xcnt = {"l": 0, "s": 0}

    def L_rb(p, k):
        r0, rows = RB[k]
        i = xcnt["l"] % 3
        xcnt["l"] += 1
        P.op("sp", lambda h, i=i, r0=r0, rows=rows: h.dma_start(out=xs[i][0:rows, :], in_=x_in[p, r0:r0 + rows, :]),
             reads=GUARD, writes=[("xs", i)], dma_sem=xs_sem[i])
        return i

    def T_rb(p, k, i):
        r0, rows = RB[k]
        tl = tiles_of(r0, r0 + rows)
        for half in range(2):
            io = next_pso()

            def ft(pe, i=i, io=io, half=half, rows=rows):
                for cc in range(4):
                    c = half * 4 + cc
                    mm = pe.transpose(out=pso[io][:, cc * 128:cc * 128 + rows], in_=xs[i][0:rows, c * 128:(c + 1) * 128],
                                      identity=ident[0:rows, 0:rows])
                return mm
            P.op("pe", ft, reads=[("xs", i), ("ident",)], writes=[("pso", io)])
            P.op("act", lambda a, io=io, half=half, r0=r0, rows=rows: a.activation(
                out=xT[:, half * 4:half * 4 + 4, r0:r0 + rows],
                in_=pso[io][:, :].rearrange("p (a b) -> p a b", a=4)[:, :, 0:rows], func=AF.Copy),
                reads=[("pso", io)], writes=[("x", half * 4 + cc, t) for cc in range(4) for t in tl])

    def S_rb(p, k):
        r0, rows = RB[k]
        tl = tiles_of(r0, r0 + rows)
        i = 3 + xcnt["s"] % 2
        xcnt["s"] += 1
        for half in range(2):
            io = next_pso()

            def ft(pe, io=io, half=half, r0=r0, rows=rows):
                for cc in range(4):
                    c = half * 4 + cc
                    mm = pe.transpose(out=pso[io][0:rows, cc * 128:(cc + 1) * 128], in_=xT[:, c, r0:r0 + rows], identity=ident[:])
                return mm
            P.op("pe", ft, reads=[("x", half * 4 + cc, t) for cc in range(4) for t in tl] + [("ident",)], writes=[("pso", io)])
            P.op("act", lambda a, i=i, io=io, half=half, rows=rows: a.activation(
                out=xs[i][0:rows, half * 512:(half + 1) * 512], in_=pso[io][0:rows, :], func=AF.Copy),
                reads=[("pso", io)] + GUARD, writes=[("xs", i, half)])
        P.op("sp", lambda h, i=i, r0=r0, rows=rows: h.dma_start(out=y_o[p, r0:r0 + rows, :], in_=xs[i][0:rows, :]),
             reads=[("xs", i, 0), ("xs", i, 1)], writes=[("xs", i)], dma_sem=xs_sem[i], final=True)

    def boundary(p_store, p_load):
        bufs = {}
        if p_load is not None:
            for k in range(3):
                bufs[k] = L_rb(p_load, k)
        for k in range(len(RB)):
            if p_store is not None:
                S_rb(p_store, k)
            if p_load is not None:
                T_rb(p_load, k, bufs[k])
                if k + 3 < len(RB):
                    bufs[k + 3] = L_rb(p_load, k + 3)

    def transposed_out(src_fn, rows, dst_aps, rkeys):
        for half in range(2):
            io = next_pso()

            def ft(pe, io=io, half=half):
                for cc in range(4):
                    mm = pe.transpose(out=pso[io][0:rows, cc * 128:(cc + 1) * 128], in_=src_fn(half * 4 + cc), identity=ident[:])
                return mm
            P.pe_fence()
            P.op("pe", ft, reads=list(rkeys) + [("ident",)], writes=[("pso", io)])
            P.pe_fence()
            P.op("act", lambda a, io=io, half=half: a.activation(out=stg[0:rows, half * 512:(half + 1) * 512], in_=pso[io][0:rows, :], func=AF.Copy),
                 reads=[("pso", io)], writes=[("stg", half)])
        for (dap, r0, nr) in dst_aps:
            P.op("sp", lambda h, dap=dap, r0=r0, nr=nr: h.dma_start(out=dap, in_=stg[r0:r0 + nr, :]),
                 reads=[("stg", 0), ("stg", 1)], writes=[("stgdma",)], dma_sem=stg_sem, final=True)

    def load_hist(src_ap, rows, dst, dkey):
        P.op("sp", lambda h: h.dma_start(out=stg[0:rows, :], in_=src_ap),
             writes=[("stg", 0), ("stg", 1)], reads=[("stgdma",)], dma_sem=stg_sem)
        for half in range(2):
            io = next_pso()

            def ft(pe, io=io, half=half):
                for cc in range(4):
                    c = half * 4 + cc
                    mm = pe.transpose(out=pso[io][:, cc * 128:cc * 128 + rows], in_=stg[0:rows, c * 128:(c + 1) * 128],
                                      identity=ident[0:rows, 0:rows])
                return mm
            P.pe_fence()
            P.op("pe", ft, reads=[("stg", 0), ("stg", 1), ("ident",)], writes=[("pso", io)])
            P.pe_fence()
            P.op("act", lambda a, io=io, half=half: a.activation(
                out=dst[:, half * 4:half * 4 + 4, :], in_=pso[io][:, :].rearrange("p (a b) -> p a b", a=4)[:, :, 0:rows], func=AF.Copy),
                reads=[("pso", io)], writes=[(dkey, half)])

    def conv_mixer(l, p, base_idx, finish, need):
        j = l // 2
        last_pass = (p == npass - 1)
        load_hist(sconv[j, p], 16, chist, "chist")
        mt_acquire()
        yield None
        for t in range(NTILE):
            if need is not None:
                yield ("need", need(t))
        kcol = lambda k, c: 192 + (j * 3 + k) * 8 + c
        for c in range(NCH):
            yield from slots_ready(base_idx + c + 1)
            s = (base_idx + c) % RING
            z = zf[c % 2]
            zs = z[:, 2 + NPR:ZFW].rearrange("p (s k) -> p s k", k=6)
            P.op("pool", lambda g_, z=z, c=c: g_.tensor_copy(out=z[:, 0:2], in_=convh[:, j, c, :]),
                 reads=[("convh", j)], writes=[("zf", c % 2, "h")])
            P.op("pool", lambda g_, zs=zs, c=c: g_.tensor_copy(out=zs[:, :, 0:2], in_=chist[:, c, :].rearrange("p (s k) -> p s k", k=2)),
                 reads=[("chist", c // 4)], writes=[("zf", c % 2, "hs")])
            for t in range(NTILE):
                cols = slice(t * TS, (t + 1) * TS)
                npr = TS if t < 2 else NPR - 2 * TS
                p0 = t * TS
                io_b = next_pso()
                i_c = next_psa()
                i_v = next_psa()
                for (bank, off) in ((pso[io_b], 0), (psa[i_c], 1024), (psa[i_v], 2048)):
                    def fm(pe, bank=bank, off=off, s=s, cols=cols):
                        for k in range(NCH):
                            mm = pe.matmul(bank[:, 0:TS], lhsT=ring[:, s, off + k * 128:off + (k + 1) * 128], rhs=hT[:, k, cols],
                                           start=(k == 0), stop=(k == NCH - 1))
                        return mm
                    wkey = ("pso", io_b) if off == 0 else (("psa", i_c) if off == 1024 else ("psa", i_v))
                    P.op("pe", fm, reads=[("h", k, t) for k in range(NCH)] + [("slot", s)], writes=[wkey])
                gi_ = (c * NTILE + t) % 2
                P.op("act", lambda a, gi_=gi_, i_c=i_c: a.activation(out=gcs[gi_], in_=psa[i_c][:, 0:TS], func=AF.Copy),
                     reads=[("psa", i_c)], writes=[("gcs", gi_)])
                P.op("dve", lambda v, z=z, gi_=gi_, i_v=i_v, p0=p0, npr=npr: v.tensor_tensor(
                    out=z[:, 2 + p0:2 + p0 + npr], in0=psa[i_v][:, 0:npr], in1=gcs[gi_][:, 0:npr], op=ALU.mult),
                    reads=[("psa", i_v), ("gcs", gi_)], writes=[("zf", c % 2, t)])
                k0, k1, k2 = kcol(0, c), kcol(1, c), kcol(2, c)
                P.op("act", lambda a, z=z, p0=p0, npr=npr, k0=k0: a.activation(out=tb[0][:, 0:npr], in_=z[:, p0:p0 + npr], func=AF.Copy,
                                                                             scale=params[:, k0:k0 + 1]),
                     reads=[("zf", c % 2, t), ("zf", c % 2, t - 1 if t > 0 else "h"), ("params",)], writes=[("tb", 0, "p")])
                P.op("dve", lambda v, z=z, p0=p0, npr=npr, k1=k1: v.scalar_tensor_tensor(
                    out=tb[1][:, 0:npr], in0=z[:, p0 + 1:p0 + 1 + npr], scalar=params[:, k1:k1 + 1], in1=tb[0][:, 0:npr],
                    op0=ALU.mult, op1=ALU.add),
                    reads=[("zf", c % 2, t), ("zf", c % 2, t - 1 if t > 0 else "h"), ("tb", 0, "p")], writes=[("tb", 1, "p")])
                P.op("dve", lambda v, z=z, p0=p0, npr=npr, k2=k2: v.scalar_tensor_tensor(
                    out=tb[2][:, 0:npr], in0=z[:, p0 + 2:p0 + 2 + npr], scalar=params[:, k2:k2 + 1], in1=tb[1][:, 0:npr],
                    op0=ALU.mult, op1=ALU.add),
                    reads=[("zf", c % 2, t), ("tb", 1, "p")], writes=[("tb", 2, "p")])
                P.op("dve", lambda v, io_b=io_b, p0=p0, npr=npr, c=c: v.tensor_tensor(
                    out=b2[:, c, p0:p0 + npr], in0=pso[io_b][:, 0:npr], in1=tb[2][:, 0:npr], op=ALU.mult),
                    reads=[("pso", io_b), ("tb", 2, "p")], writes=[("b2", c, t)])
                if t == NTILE - 1:
                    sl = slice(npr, TS)
                    P.op("dve", lambda v, zs=zs, gi_=gi_, i_v=i_v, sl=sl: v.tensor_tensor(
                        out=zs[:, :, 2:6], in0=v3(psa[i_v][:, sl]), in1=v3(gcs[gi_][:, sl]), op=ALU.mult),
                        reads=[("psa", i_v), ("gcs", gi_)], writes=[("zf", c % 2, "s")])
                    P.op("act", lambda a, zs=zs, k0=k0, sl=sl: a.activation(out=v3(tb[0][:, sl]), in_=zs[:, :, 0:4], func=AF.Copy,
                                                                    scale=params[:, k0:k0 + 1]),
                         reads=[("zf", c % 2, "s"), ("zf", c % 2, "hs"), ("params",)], writes=[("tb", 0, "s")])
                    P.op("dve", lambda v, zs=zs, k1=k1, sl=sl: v.scalar_tensor_tensor(
                        out=v3(tb[1][:, sl]), in0=zs[:, :, 1:5], scalar=params[:, k1:k1 + 1], in1=v3(tb[0][:, sl]),
                        op0=ALU.mult, op1=ALU.add),
                        reads=[("zf", c % 2, "s"), ("zf", c % 2, "hs"), ("tb", 0, "s")], writes=[("tb", 1, "s")])
                    P.op("dve", lambda v, zs=zs, k2=k2, sl=sl: v.scalar_tensor_tensor(
                        out=v3(tb[2][:, sl]), in0=zs[:, :, 2:6], scalar=params[:, k2:k2 + 1], in1=v3(tb[1][:, sl]),
                        op0=ALU.mult, op1=ALU.add),
                        reads=[("zf", c % 2, "s"), ("tb", 1, "s")], writes=[("tb", 2, "s")])
                    P.op("dve", lambda v, io_b=io_b, c=c, sl=sl: v.tensor_tensor(
                        out=b2[:, c, NPR:NTOK], in0=pso[io_b][:, sl], in1=tb[2][:, sl], op=ALU.mult),
                        reads=[("pso", io_b), ("tb", 2, "s")], writes=[("b2", c, t)])
                yield None
            P.op("pool", lambda g_, z=z, c=c: g_.tensor_copy(out=convh[:, j, c, :], in_=z[:, NPR:NPR + 2]),
                 reads=[("zf", c % 2, 2)], writes=[("convh", j)])
            P.op("pool", lambda g_, zs=zs, c=c: g_.tensor_copy(out=csamp[:, c, :].rearrange("p (s k) -> p s k", k=2), in_=zs[:, :, 4:6]),
                 reads=[("zf", c % 2, "s")], writes=[("csamp",)])
            release([base_idx + c])
        transposed_out(lambda c: csamp[:, c, :], 16, [(cs_s[j, p], 0, 16)], [("csamp",)])
        if last_pass:
            transposed_out(lambda c: convh[:, j, c, :], 2, [(cs_p[j], 0, 2)], [("convh", j)])
        yield from slots_ready(base_idx + 11)
        wslots = [(base_idx + 8 + i) % RING for i in range(3)]
        for t in range(NTILE):
            cols = slice(t * TS, (t + 1) * TS)
            bkeys = [("b2", cc, t) for cc in range(NCH)]
            for dc in range(NCH):
                io = next_pso()

                def fw(pe, dc=dc, io=io, cols=cols):
                    for cc in range(NCH):
                        mm = pe.matmul(pso[io][:, 0:TS], lhsT=ring[:, wslots[cc // 3], (cc % 3) * 1024 + dc * 128:(cc % 3) * 1024 + (dc + 1) * 128],
                                       rhs=b2[:, cc, cols], start=(cc == 0), stop=(cc == NCH - 1))
                    return mm
                P.op("pe", fw, reads=bkeys + [("slot", s_) for s_ in wslots], writes=[("pso", io)])
                P.op("act", lambda a, dc=dc, io=io, cols=cols: a.activation(out=acc[:, dc, cols], in_=pso[io][:, 0:TS], func=AF.Copy),
                     reads=[("pso", io)], writes=[("acc", dc, t)])
                yield None
            if t == NTILE - 1:
                release(range(base_idx + 8, base_idx + 11))
                mt_release()
            finish(t)

    def pool_mixer(l, p, base_idx, finish, need):
        j = l // 2
        last_pass = (p == npass - 1)
        s = base_idx % RING
        load_hist(spool[j, p].rearrange("s r d -> (s r) d"), NSS * 15, phist, "phist")
        P.op("sp", lambda h: h.dma_start(out=ps_s[j, p, :, 0:11, :], in_=spool[j, p, :, 4:15, :]), dma_sem=h2h_sem, final=True)
        mt_acquire()
        yield None
        yield from slots_ready(base_idx + 1)
        kc = {"k": 0}
        for t in range(NTILE):
            if need is not None:
                yield ("need", need(t))
            cols = slice(t * TS, (t + 1) * TS)
            p0 = t * TS
            npr = TS if t < 2 else NPR - 2 * TS
            smp = (t == NTILE - 1)
            W = 15 + npr + (NSS * 19 if smp else 0)
            if smp:
                ukeys = [("acc", c, 2) for c in range(NCH)]
                transposed_out(lambda c: acc[:, c, NPR:NTOK], NSM, [(ps_s[j, p, s_, 11:15, :], 4 * s_, 4) for s_ in range(NSS)], ukeys)
                yield None
            for g in range(4):
                w = 2 << g
                for c in (2 * g, 2 * g + 1):
                    k = kc["k"] % 2
                    kc["k"] += 1
                    bu = uft[k]
                    eng = "pool" if c % 2 == 0 else "dve"
                    P.op("pool", lambda g_, bu=bu, c=c: g_.tensor_copy(out=bu[:, 0:15], in_=poolh[:, j, c, :]),
                         reads=[("poolh", j, c)], writes=[("uft", k, "h")])
                    P.op("act", lambda a, bu=bu, c=c, p0=p0, npr=npr: a.activation(out=bu[:, 15:15 + npr], in_=acc[:, c, p0:p0 + npr], func=AF.Copy),
                         reads=[("acc", c, t)], writes=[("uft", k, "u")])
                    ufk = [("uft", k, "h"), ("uft", k, "u")]
                    if smp:
                        us = bu[:, 15 + npr:W].rearrange("p (s k) -> p s k", k=19)
                        P.op("pool", lambda g_, us=us, c=c: g_.tensor_copy(out=us[:, :, 0:15], in_=phist[:, c, :].rearrange("p (s k) -> p s k", k=15)),
                             reads=[("phist", c // 4)], writes=[("uft", k, "hs")])
                        P.op("pool", lambda g_, us=us, c=c: g_.tensor_copy(out=us[:, :, 15:19], in_=acc[:, c, NPR:NTOK].rearrange("p (s k) -> p s k", k=4)),
                             reads=[("acc", c, 2)], writes=[("uft", k, "us")])
                        ufk = ufk + [("uft", k, "hs"), ("uft", k, "us")]
                    src, skeys = bu, ufk
                    sh = 1
                    bi = 0
                    while sh < w:
                        dst = wbuf[eng][bi]
                        dk = ("wbuf", eng, bi)
                        lo = 2 * sh - 1
                        P.op(eng, lambda v, dst=dst, src=src, lo=lo, sh=sh, W=W: v.tensor_tensor(
                            out=dst[:, lo:W], in0=src[:, lo:W], in1=src[:, lo - sh:W - sh], op=ALU.add),
                            reads=skeys, writes=[dk])
                        src, skeys = dst, [dk]
                        sh *= 2
                        bi ^= 1
                    win = src
                    P.op("act", lambda a, bu=bu, c=c, npr=npr: a.activation(out=poolh[:, j, c, :], in_=bu[:, npr:npr + 15], func=AF.Copy),
                         reads=ufk, writes=[("poolh", j, c)])
                    P.op("dve", lambda v, win=win, bu=bu, c=c, w=w, p0=p0, npr=npr: v.scalar_tensor_tensor(
                        out=b2[:, c, p0:p0 + npr], in0=win[:, 15:15 + npr], scalar=1.0 / w, in1=bu[:, 15:15 + npr],
                        op0=ALU.mult, op1=ALU.subtract),
                        reads=skeys + ufk, writes=[("b2", c, t)])
                    if smp:
                        wins = win[:, 15 + npr:W].rearrange("p (s k) -> p s k", k=19)
                        P.op("dve", lambda v, wins=wins, us=us, c=c, w=w: v.scalar_tensor_tensor(
                            out=b2[:, c, NPR:NTOK].rearrange("p (s k) -> p s k", k=4), in0=wins[:, :, 15:19], scalar=1.0 / w, in1=us[:, :, 15:19],
                            op0=ALU.mult, op1=ALU.subtract),
                            reads=skeys + ufk, writes=[("b2", c, t)])
                    if p == 0 and t == 0:
                        P.op("dve", lambda v, win=win, g=g: v.tensor_tensor(out=tmpf[:, 0:15], in0=win[:, 15:30], in1=invc[:, g, 0:15], op=ALU.mult),
                             reads=skeys + [("invc", g)], writes=[("tmpf",)])
                        P.op("dve", lambda v, bu=bu, c=c: v.tensor_tensor(out=b2[:, c, 0:15], in0=tmpf[:, 0:15], in1=bu[:, 15:30], op=ALU.subtract),
                             reads=[("tmpf",)] + ufk, writes=[("b2", c, t)])
                    yield None
                for dsub in range(2):
                    io = next_pso()
                    dc = 2 * g + dsub

                    def fp(pe, io=io, g=g, dsub=dsub, cols=cols):
                        for cc in range(2):
                            o_ = (g * 2 + cc) * 256 + dsub * 128
                            mm = pe.matmul(pso[io][:, 0:TS], lhsT=ring[:, s, o_:o_ + 128], rhs=b2[:, 2 * g + cc, cols],
                                           start=(cc == 0), stop=(cc == 1))
                        return mm
                    P.op("pe", fp, reads=[("b2", 2 * g, t), ("b2", 2 * g + 1, t), ("slot", s)], writes=[("pso", io)])
                    sc = 240 + j * 8 + dc
                    P.op("dve", lambda v, io=io, dc=dc, sc=sc, cols=cols: v.tensor_scalar(out=acc[:, dc, cols], in0=pso[io][:, 0:TS],
                                                                                         scalar1=params[:, sc:sc + 1], scalar2=None, op0=ALU.mult),
                         reads=[("pso", io), ("params",)], writes=[("acc", dc, t)])
                yield None
            if t == NTILE - 1:
                release([base_idx])
                if last_pass:
                    transposed_out(lambda c: poolh[:, j, c, :], 15, [(ps_p[j], 0, 15)], [("poolh", j, c) for c in range(NCH)])
                mt_release()
            finish(t)

    def run_pipeline(gens, kinds):
        n = len(gens)
        waiting = [None] * n
        finished = [False] * n

        def advance(i, primary):
            w_ = waiting[i]
            if w_ is not None:
                if w_[0] == "need":
                    if primary:
                        P.flush_through(w_[1])
                    elif w_[1] not in P.done:
                        return
                elif w_[0] == "slot":
                    if ws["issued"] < min(w_[1], ws["total"]):
                        assert not primary, ("primary blocked on slot", w_, ws)
                        return
                waiting[i] = None
            try:
                r = next(gens[i])
            except StopIteration:
                finished[i] = True
                return
            if r is not None:
                waiting[i] = r

        cur = 0
        while cur < n:
            advance(cur, True)
            if finished[cur]:
                cur += 1
                continue
            if cfg.get("interleave", True) and cur + 1 < n and not (kinds[cur] == "ffn" and kinds[cur + 1] == "ffn"):
                advance(cur + 1, False)
                assert not finished[cur + 1]
            P.drain_step()

    for p in range(npass):
        if p == 0:
            boundary(None, 0)
            ws["extra"] = [("xs", i) for i in range(3)]
            issue_slot_loads(RING)
            ws["extra"] = []
        idx = p * SPP
        subs = []
        for l in range(n_layers):
            subs.append(("ffn", l, 0))
            subs.append(("mix", l, 0))
            subs.append(("ffn", l, 1))
        subs = subs[:cfg.get("nsub", 12)]

        def pre_of(si, t):
            kind, l, s_ = subs[si]
            if kind == "ffn":
                prenorm(l, 0 if s_ == 0 else 4, t)
            else:
                prenorm(l, 2, t, to_u=(l % 2 == 1))

        for t in range(NTILE):
            if subs:
                pre_of(0, t)
        gens, kinds = [], []
        for si, (kind, l, s_) in enumerate(subs):
            def finish(t, si=si, kind=kind, l=l, s_=s_):
                P.defer = True
                if kind == "ffn":
                    postnorm(l, 1 if s_ == 0 else 5, t, half=True)
                else:
                    postnorm(l, 3, t, half=False)
                if si + 1 < len(subs):
                    pre_of(si + 1, t)
                P.defer = False
                P.mark(("chain", p, si, t))
            need = (lambda t, si=si: ("chain", p, si - 1, t)) if si > 0 else None
            if kind == "ffn":
                prev = kinds[-1] if si > 0 else None
                if prev == "conv":
                    gens.append(ffn(idx, finish, need, groups=[5, 2, 5, 5, 5], prefix=2))
                else:
                    gens.append(ffn(idx, finish, need, groups=[5, 4, 4, 4, 5], prefix=2))
                kinds.append("ffn")
                idx += NFC
            elif l % 2 == 0:
                gens.append(conv_mixer(l, p, idx, finish, need))
                kinds.append("conv")
                idx += 11
            else:
                gens.append(pool_mixer(l, p, idx, finish, need))
                kinds.append("pool")
                idx += 1
        run_pipeline(gens, kinds)
        P.flush_all()
        boundary(p, p + 1 if p + 1 < npass else None)

    P.emit()
    return nc


def _kblock(w):
    return w.reshape(8, 128, w.shape[1]).transpose(1, 0, 2).reshape(128, 8 * w.shape[1])


def build_wstream(ffn_w_gate, ffn_w_up, ffn_w_down, conv_w_in, conv_w_out, pool_w_group):
    ws = np.zeros((SLOTS_FULL, 128, SLOTW), np.float32)
    i = 0
    for l in range(4):
        for s in range(2):
            if s == 1:
                j = l // 2
                if l % 2 == 0:
                    for c in range(8):
                        for part in range(3):
                            ws[i, :, part * 1024:(part + 1) * 1024] = _kblock(conv_w_in[j][:, part * 1024 + c * 128:part * 1024 + (c + 1) * 128])
                        i += 1
                    wo = conv_w_out[j].reshape(8, 128, 1024).transpose(1, 0, 2).reshape(128, 8192)
                    ws[i, :, :] = wo[:, 0:3072]
                    ws[i + 1, :, :] = wo[:, 3072:6144]
                    ws[i + 2, :, 0:2048] = wo[:, 6144:8192]
                    i += 3
                else:
                    ws[i, :, 0:2048] = pool_w_group[j].reshape(4, 2, 128, 256).transpose(2, 0, 1, 3).reshape(128, 2048)
                    i += 1
            for jf in range(NFC):
                ws[i, :, 0:1024] = _kblock(ffn_w_gate[l, s][:, jf * 128:(jf + 1) * 128])
                ws[i, :, 1024:2048] = _kblock(ffn_w_up[l, s][:, jf * 128:(jf + 1) * 128])
                ws[i, :, 2048:3072] = ffn_w_down[l, s][jf * 128:(jf + 1) * 128, :]
                i += 1
    assert i == SLOTS_FULL
    return ws


def build_params(norm_gains, conv_kernel, pool_scale):
    pr = np.zeros((128, 256), np.float32)
    pr[:, 0:192] = norm_gains.reshape(24, 8, 128).transpose(2, 0, 1).reshape(128, 192)
    pr[:, 192:240] = conv_kernel.reshape(6, 8, 128).transpose(2, 0, 1).reshape(128, 48)
    pr[:, 240:256] = pool_scale.reshape(2, 8, 128).transpose(2, 0, 1).reshape(128, 16)
    return pr


_NC_CACHE = {}


def kernel(x_prompt, x_sample, state_conv, state_pool, norm_gains, ffn_w_gate, ffn_w_up, ffn_w_down,
           conv_w_in, conv_kernel, conv_w_out, pool_w_group, pool_scale):
    f = lambda a: np.asarray(a, dtype=np.float32)
    x_prompt, x_sample, state_conv, state_pool = f(x_prompt), f(x_sample), f(state_conv), f(state_pool)
    ws = build_wstream(f(ffn_w_gate), f(ffn_w_up), f(ffn_w_down), f(conv_w_in), f(conv_w_out), f(pool_w_group))
    pr = build_params(f(norm_gains), f(conv_kernel), f(pool_scale))
    ident = np.eye(128, dtype=np.float32)
    ws_used = np.ascontiguousarray(ws[:slots_per_pass(CFG['n_layers'])])
    in_maps = []
    for i in range(N_CORES):
        xin = np.empty((NPASS, NTOK, D), np.float32)
        sc = np.empty((2, NPASS, 16, D), np.float32)
        sp = np.empty((2, NPASS, NSS, 15, D), np.float32)
        for p in range(NPASS):
            xin[p, 0:NPR] = x_prompt[i, p * NPR:(p + 1) * NPR]
            b0 = i * 16 + p * NSS
            xin[p, NPR:] = x_sample[b0:b0 + NSS].reshape(NSM, D)
            sc[:, p] = state_conv[:, b0:b0 + NSS].reshape(2, 16, D)
            sp[:, p] = state_pool[:, b0:b0 + NSS]
        in_maps.append({"x_in": xin, "sconv": sc, "spool": sp, "wstream": ws_used, "params": pr, "ident": ident})
    key = (CFG["n_layers"], CFG["npass"], CFG.get("nsub", 12))
    if key not in _NC_CACHE:
        _NC_CACHE[key] = build_program(CFG)
    nc = _NC_CACHE[key]
    res = run_bass_kernel_spmd(nc, in_maps, core_ids=list(range(N_CORES)))
    R = res.results
    y_prompt = np.empty((8, 2048, D), np.float32)
    y_sample = np.empty((128, 4, D), np.float32)
    ncp = np.empty((2, 8, 2, D), np.float32)
    ncs = np.empty((2, 128, 2, D), np.float32)
    npp = np.empty((2, 8, 15, D), np.float32)
    nps = np.empty((2, 128, 15, D), np.float32)
    for i in range(N_CORES):
        r = R[i]
        for p in range(NPASS):
            b0 = i * 16 + p * NSS
            y_prompt[i, p * NPR:(p + 1) * NPR] = r["y_o"][p, 0:NPR]
            y_sample[b0:b0 + NSS] = r["y_o"][p, NPR:].reshape(NSS, 4, D)
            ncs[:, b0:b0 + NSS] = r["cs_s"][:, p].reshape(2, NSS, 2, D)
            nps[:, b0:b0 + NSS] = r["ps_s"][:, p]
        ncp[:, i] = r["cs_p"]
        npp[:, i] = r["ps_p"]
    return (y_prompt, y_sample, ncp, ncs, npp, nps)
```
